# Optimizing a Trainium2 kernel written in Bass

```python
import jax, jax.numpy as jnp
from jax import lax
import numpy as np

D_MODEL = 1024
BATCH = 8
SEQ = 2048
DEPTH = 1
DEC_BATCH = 128
DEC_SEQ = 8
PAST_LEN = 16384
PAGE_SIZE = 128

HEAD_DIM = 128
N_RET_HEADS = 4
N_ML_HEADS = 4
RET_W = N_RET_HEADS * HEAD_DIM
ML_W = N_ML_HEADS * HEAD_DIM
MIX_W = RET_W + ML_W
IN_COLS = 4 * RET_W + 4 * ML_W + 2 * N_ML_HEADS
D_FF = 2816
CONV_W = 3
CHUNK = 128
ROPE_BASE = 10000.0
EPS = 1e-6
M_INIT = -1e30

kernel_name = "retnet_mlstm_parallel_heads_convffn_step"


def rmsnorm(x, g):
    xf = x.astype(jnp.float32)
    y = xf * lax.rsqrt(jnp.mean(xf * xf, axis=-1, keepdims=True) + EPS)
    return (y * g.astype(jnp.float32)).astype(x.dtype)


def head_groupnorm(h, g):
    mu = jnp.mean(h, axis=-1, keepdims=True)
    var = jnp.mean(jnp.square(h - mu), axis=-1, keepdims=True)
    y = (h - mu) * lax.rsqrt(var + EPS)
    B, L, H, D = h.shape
    return y.reshape(B, L, H * D) * g.astype(jnp.float32)


def rotary(x, pos):
    D = x.shape[-1]
    freqs = ROPE_BASE ** (-jnp.arange(0, D, 2, dtype=jnp.float32) / D)
    ang = pos.astype(jnp.float32)[:, None] * freqs[None, :]
    cos = jnp.cos(ang)[None, :, None, :]
    sin = jnp.sin(ang)[None, :, None, :]
    x1, x2 = x[..., : D // 2], x[..., D // 2:]
    return jnp.concatenate([x1 * cos - x2 * sin, x1 * sin + x2 * cos], axis=-1)


def to_chunks(x, c):
    B, L, H = x.shape[:3]
    x = x.reshape((B, L // c, c, H) + x.shape[3:])
    return jnp.swapaxes(jnp.moveaxis(x, 1, 0), 2, 3)


def from_chunks(x):
    nc, B, H, c, D = x.shape
    x = jnp.moveaxis(jnp.swapaxes(x, 2, 3), 0, 1)
    return x.reshape(B, nc * c, H, D)


def retention_chunked(q, k, v, S0):
    L = q.shape[1]
    H = q.shape[2]
    c = min(CHUNK, L)
    lg = jnp.log(1.0 - 2.0 ** (-5.0 - jnp.arange(H, dtype=jnp.float32)))
    idx = jnp.arange(c)
    causal = idx[:, None] >= idx[None, :]
    expo = (idx[:, None] - idx[None, :]).astype(jnp.float32)[None] * lg[:, None, None]
    dmat = jnp.where(causal[None], jnp.exp(jnp.where(causal[None], expo, 0.0)), 0.0)
    xi = jnp.exp((idx + 1).astype(jnp.float32)[None, :] * lg[:, None])
    zeta = jnp.exp((c - 1 - idx).astype(jnp.float32)[None, :] * lg[:, None])
    chunk_decay = jnp.exp(c * lg)

    def step(S, inp):
        qc, kc, vc = inp
        scores = jnp.einsum('bhnd,bhmd->bhnm', qc, kc) * dmat[None]
        out = (jnp.einsum('bhnm,bhmv->bhnv', scores, vc)
               + jnp.einsum('bhnd,bhdv->bhnv', qc, S) * xi[None, :, :, None])
        S_new = (S * chunk_decay[None, :, None, None]
                 + jnp.einsum('bhmd,bhmv->bhdv', kc * zeta[None, :, :, None], vc))
        return S_new, out

    S_fin, outs = lax.scan(step, S0, (to_chunks(q, c), to_chunks(k, c), to_chunks(v, c)))
    return from_chunks(outs), S_fin


def mlstm_chunked(q, k, v, ig, fg, C0, n0, m0):
    L = q.shape[1]
    c = min(CHUNK, L)
    logf = jax.nn.log_sigmoid(fg)
    idx = jnp.arange(c)
    causal = idx[:, None] >= idx[None, :]
    ig_c = jnp.moveaxis(ig.reshape(ig.shape[0], L // c, c, ig.shape[2]), 1, 0).swapaxes(2, 3)
    lf_c = jnp.moveaxis(logf.reshape(logf.shape[0], L // c, c, logf.shape[2]), 1, 0).swapaxes(2, 3)

    def step(carry, inp):
        C, n, m = carry
        qc, kc, vc, ic, lfc = inp
        b = jnp.cumsum(lfc, axis=-1)
        logw = b[..., :, None] - b[..., None, :] + ic[..., None, :]
        logw = jnp.where(causal, logw, -jnp.inf)
        m_t = jnp.maximum(b + m[..., None], jnp.max(logw, axis=-1))
        w = jnp.exp(logw - m_t[..., None])
        inter = jnp.exp(b + m[..., None] - m_t)
        s = jnp.einsum('bhtd,bhsd->bhts', qc, kc) * w
        num = (jnp.einsum('bhts,bhsv->bhtv', s, vc)
               + inter[..., None] * jnp.einsum('bhvd,bhtd->bhtv', C, qc))
        den_dot = jnp.sum(s, axis=-1) + inter * jnp.einsum('bhd,bhtd->bht', n, qc)
        den = jnp.maximum(jnp.abs(den_dot), jnp.exp(-m_t))
        h = num / den[..., None]
        m_new = m_t[..., -1]
        wl = jnp.exp(b[..., -1:] - b + ic - m_new[..., None])
        decay = jnp.exp(b[..., -1] + m - m_new)
        C_new = decay[..., None, None] * C + jnp.einsum('bhsv,bhsd->bhvd', vc * wl[..., None], kc)
        n_new = decay[..., None] * n + jnp.einsum('bhs,bhsd->bhd', wl, kc)
        return (C_new, n_new, m_new), h

    (C_f, n_f, m_f), outs = lax.scan(
        step, (C0, n0, m0),
        (to_chunks(q, c), to_chunks(k, c), to_chunks(v, c), ig_c, lf_c))
    return from_chunks(outs), C_f, n_f, m_f


def decoder_layer(x, pos, S_ret, C, n, m, conv_buf,
                  pre_mix_g, w_in, b_g, ret_g, ml_g, w_out, post_mix_g,
                  pre_ffn_g, w_up, conv_w, conv_b, w_down, post_ffn_g):
    B, L, _ = x.shape
    f32 = jnp.float32
    h = rmsnorm(x, pre_mix_g)
    proj = (h @ w_in).astype(f32)
    o = 0
    def take(width):
        nonlocal o
        seg = proj[..., o:o + width]
        o += width
        return seg
    q_r = take(RET_W).reshape(B, L, N_RET_HEADS, HEAD_DIM)
    k_r = take(RET_W).reshape(B, L, N_RET_HEADS, HEAD_DIM)
    v_r = take(RET_W).reshape(B, L, N_RET_HEADS, HEAD_DIM)
    g_r = take(RET_W)
    q_m = take(ML_W).reshape(B, L, N_ML_HEADS, HEAD_DIM)
    k_m = take(ML_W).reshape(B, L, N_ML_HEADS, HEAD_DIM)
    v_m = take(ML_W).reshape(B, L, N_ML_HEADS, HEAD_DIM)
    o_m = take(ML_W)
    gates = take(2 * N_ML_HEADS) + b_g.astype(f32)
    i_pre, f_pre = gates[..., :N_ML_HEADS], gates[..., N_ML_HEADS:]

    scale = HEAD_DIM ** -0.5
    q_r = rotary(q_r, pos)
    k_r = rotary(k_r, pos) * scale
    ret_out, S_new = retention_chunked(q_r, k_r, v_r, S_ret.astype(f32))
    ret_y = head_groupnorm(ret_out, ret_g) * jax.nn.silu(g_r)
    ml_out, C_new, n_new, m_new = mlstm_chunked(
        q_m, k_m * scale, v_m, i_pre, f_pre,
        C.astype(f32), n.astype(f32), m.astype(f32))
    ml_y = head_groupnorm(ml_out, ml_g) * jax.nn.sigmoid(o_m)

    mix = jnp.concatenate([ret_y, ml_y], axis=-1).astype(x.dtype) @ w_out
    x = x + rmsnorm(mix, post_mix_g)

    h2 = rmsnorm(x, pre_ffn_g)
    up = h2 @ w_up
    padded = jnp.concatenate([conv_buf.astype(up.dtype), up], axis=1)
    conv = sum(padded[:, j:j + L] * conv_w[j] for j in range(CONV_W)) + conv_b
    new_buf = padded[:, L:]
    gate, val = conv[..., :D_FF], conv[..., D_FF:]
    ffn = (jax.nn.gelu(gate.astype(f32), approximate=True) * val.astype(f32)).astype(x.dtype) @ w_down
    x = x + rmsnorm(ffn, post_ffn_g)
    return x, S_new, C_new, n_new, m_new, new_buf


def setup_inputs(seed: int = 0) -> dict:
    key = jax.random.key(seed)
    ks = jax.random.split(key, 24)
    f32 = jnp.float32
    nrm = lambda k, shape, s: jax.random.normal(k, shape, f32) * s
    gain = lambda k, shape: 1.0 + 0.05 * jax.random.normal(k, shape, f32)
    f_bias = jnp.linspace(3.0, 6.0, N_ML_HEADS, dtype=f32)
    b_gates = jnp.concatenate([
        nrm(ks[0], (DEPTH, N_ML_HEADS), 0.01),
        f_bias[None] + nrm(ks[1], (DEPTH, N_ML_HEADS), 0.01)], axis=-1)
    return {
        "x_prompt": nrm(ks[2], (BATCH, SEQ, D_MODEL), 1.0),
        "x_sample": nrm(ks[3], (DEC_BATCH, DEC_SEQ, D_MODEL), 1.0),
        "state_ret": nrm(ks[4], (DEPTH, DEC_BATCH, N_RET_HEADS, HEAD_DIM, HEAD_DIM), 0.1),
        "state_mlstm_C": nrm(ks[5], (DEPTH, DEC_BATCH, N_ML_HEADS, HEAD_DIM, HEAD_DIM), 0.1),
        "state_mlstm_n": nrm(ks[6], (DEPTH, DEC_BATCH, N_ML_HEADS, HEAD_DIM), 0.1),
        "state_mlstm_m": nrm(ks[7], (DEPTH, DEC_BATCH, N_ML_HEADS), 1.0),
        "cache_ffn_conv": nrm(ks[8], (DEPTH, DEC_BATCH, CONV_W - 1, 2 * D_FF), 1.0),
        "pre_mix_gain": gain(ks[9], (DEPTH, D_MODEL)),
        "w_in": nrm(ks[10], (DEPTH, D_MODEL, IN_COLS), D_MODEL ** -0.5),
        "b_gates": b_gates,
        "ret_head_gain": gain(ks[11], (DEPTH, RET_W)),
        "mlstm_head_gain": gain(ks[12], (DEPTH, ML_W)),
        "w_out": nrm(ks[13], (DEPTH, MIX_W, D_MODEL), MIX_W ** -0.5),
        "post_mix_gain": gain(ks[14], (DEPTH, D_MODEL)),
        "pre_ffn_gain": gain(ks[15], (DEPTH, D_MODEL)),
        "w_up": nrm(ks[16], (DEPTH, D_MODEL, 2 * D_FF), D_MODEL ** -0.5),
        "conv_w": nrm(ks[17], (DEPTH, CONV_W, 2 * D_FF), CONV_W ** -0.5),
        "conv_b": nrm(ks[18], (DEPTH, 2 * D_FF), 0.02),
        "w_down": nrm(ks[19], (DEPTH, D_FF, D_MODEL), D_FF ** -0.5),
        "post_ffn_gain": gain(ks[20], (DEPTH, D_MODEL)),
    }


def reference(x_prompt, x_sample, state_ret, state_mlstm_C, state_mlstm_n, state_mlstm_m,
              cache_ffn_conv, pre_mix_gain, w_in, b_gates, ret_head_gain, mlstm_head_gain,
              w_out, post_mix_gain, pre_ffn_gain, w_up, conv_w, conv_b, w_down, post_ffn_gain):
    f32 = jnp.float32
    Bp, Lp, _ = x_prompt.shape
    Ls = x_sample.shape[1]
    pos_p = jnp.arange(Lp, dtype=jnp.int32)
    pos_s = PAST_LEN + jnp.arange(Ls, dtype=jnp.int32)
    yp, ys = x_prompt, x_sample
    out_Sp, out_Ss, out_Cp, out_Cs = [], [], [], []
    out_np, out_ns, out_mp, out_ms, out_bp, out_bs = [], [], [], [], [], []
    for l in range(DEPTH):
        params = (pre_mix_gain[l], w_in[l], b_gates[l], ret_head_gain[l], mlstm_head_gain[l],
                  w_out[l], post_mix_gain[l], pre_ffn_gain[l], w_up[l], conv_w[l], conv_b[l],
                  w_down[l], post_ffn_gain[l])
        S0 = jnp.zeros((Bp, N_RET_HEADS, HEAD_DIM, HEAD_DIM), f32)
        C0 = jnp.zeros((Bp, N_ML_HEADS, HEAD_DIM, HEAD_DIM), f32)
        n0 = jnp.zeros((Bp, N_ML_HEADS, HEAD_DIM), f32)
        m0 = jnp.full((Bp, N_ML_HEADS), M_INIT, f32)
        buf0 = jnp.zeros((Bp, CONV_W - 1, 2 * D_FF), x_prompt.dtype)
        yp, Sp, Cp, np_, mp, bp = decoder_layer(yp, pos_p, S0, C0, n0, m0, buf0, *params)
        ys, Ss, Cs, ns, ms, bs = decoder_layer(ys, pos_s, state_ret[l], state_mlstm_C[l],
                                               state_mlstm_n[l], state_mlstm_m[l],
                                               cache_ffn_conv[l], *params)
        out_Sp.append(Sp); out_Ss.append(Ss)
        out_Cp.append(Cp); out_Cs.append(Cs)
        out_np.append(np_); out_ns.append(ns)
        out_mp.append(mp); out_ms.append(ms)
        out_bp.append(bp); out_bs.append(bs)
    return (yp, ys,
            jnp.stack(out_Sp), jnp.stack(out_Ss),
            jnp.stack(out_Cp), jnp.stack(out_Cs),
            jnp.stack(out_np), jnp.stack(out_ns),
            jnp.stack(out_mp), jnp.stack(out_ms),
            jnp.stack(out_bp), jnp.stack(out_bs))
```

```python
import contextlib
import numpy as np
import concourse.bass as bass
import concourse.mybir as mybir
from concourse.bass_utils import run_bass_kernel_spmd
from concourse.alu_op_type import AluOpType as ALU

F32 = mybir.dt.float32
BF16 = mybir.dt.bfloat16
AF = mybir.ActivationFunctionType
AXX = mybir.AxisListType.X

D = 1024
NT = 17
ST = 16
INC = 4104
DFF = 2816
NJ = 22
EPS = 1e-6
SCALE = 128 ** -0.5
GAM = [1.0 - 2.0 ** (-5.0 - h) for h in range(4)]

C_ID = 0
C_MT = 128
C_SEL = 256
C_R = 384
C_XI = 896
C_Z = 1408
C_RM = 1412
C_FI = 1428
NCONST = 1444

DEBUG = {}


class T:
    __slots__ = ("name", "w", "r")

    def __init__(self, name=""):
        self.name = name
        self.w = None
        self.r = {}


class Eng:
    def __init__(self, name, obj, sem, sig_all):
        self.name, self.obj, self.sem, self.sig_all = name, obj, sem, sig_all
        self.count = 0
        self.seen = {}
        self.pending = False


class Sched:
    def __init__(self, nc, es):
        self.nc = nc
        mk = lambda n: es.enter_context(nc.semaphore(n))
        self.pe = Eng("pe", nc.tensor, mk("s_pe"), False)
        self.act = Eng("act", nc.scalar, mk("s_act"), True)
        self.dve = Eng("dve", nc.vector, mk("s_dve"), True)
        self.pool = Eng("pool", nc.gpsimd, mk("s_pool"), True)
        self.sp = Eng("sp", nc.sync, None, False)
        self.engs = [self.pe, self.act, self.dve, self.pool, self.sp]
        self.q_sp = dict(eng=self.sp, sems=[mk("q_sp%d" % i) for i in range(16)], issued=[0] * 16, n=0, id=0)
        self.q_pool = dict(eng=self.pool, sems=[mk("q_po%d" % i) for i in range(12)], issued=[0] * 12, n=0, id=1)
        self.qs = [self.q_sp, self.q_pool]

    label = ""
    pe_log = []
    pe_waits = []
    prod = {}

    def _wait(self, e, sem, val):
        if e.seen.get(id(sem), 0) < val:
            e.obj.wait_ge(sem, val)
            e.seen[id(sem)] = val
            if e is self.pe:
                Sched.pe_waits.append((len(Sched.pe_log), Sched.prod.get((id(sem), val), "dma/?")))

    def _deps(self, e, r, w):
        waits = {}

        def need(rec):
            if rec is None:
                return
            sem, val = rec
            if e is self.pe and sem is self.pe.sem:
                return
            if waits.get(id(sem), (None, 0))[1] < val:
                waits[id(sem)] = (sem, val)
        for b in r:
            need(b.w)
        for b in w:
            need(b.w)
            for rec in b.r.values():
                need(rec)
        for sem, val in waits.values():
            self._wait(e, sem, val)

    def op(self, e, fn, r=(), w=(), sig=None):
        self._deps(e, r, w)
        inst = fn(e.obj)
        signal = e.sig_all if sig is None else sig
        if e is self.pe:
            Sched.pe_log.append(Sched.label)
        if signal:
            e.count += 1
            inst.then_inc(e.sem, 1)
            rec = (e.sem, e.count)
            e.pending = False
            Sched.prod[(id(e.sem), e.count)] = e.name + ":" + Sched.label
        else:
            rec = (e.sem, e.count + 1)
            e.pending = True
        for b in w:
            b.w = rec
            b.r = {}
        for b in r:
            b.r[e.name] = rec
        return inst

    def dma(self, q, out, in_, r=(), w=(), **kw):
        e = q["eng"]
        slot = q["n"] % len(q["sems"])
        q["n"] += 1
        sem = q["sems"][slot]
        if q["issued"][slot] > 0:
            self._wait(e, sem, 16 * q["issued"][slot])
        self._deps(e, r, w)
        e.obj.dma_start(out=out, in_=in_, **kw).then_inc(sem, 16)
        q["issued"][slot] += 1
        rec = (sem, 16 * q["issued"][slot])
        for b in w:
            b.w = rec
            b.r = {}
        for b in r:
            b.r["q%d_%d" % (q["id"], slot)] = rec

    def barrier(self, dma=True):
        assert not self.pe.pending
        for e in self.engs:
            for f in self.engs:
                if f is not e and f.sem is not None and f.count > 0:
                    self._wait(e, f.sem, f.count)
            if dma:
                for q in self.qs:
                    for sem, n in zip(q["sems"], q["issued"]):
                        if n > 0:
                            self._wait(e, sem, 16 * n)


class B:
    def __init__(self, ap, name=""):
        self.ap = ap
        self.t = T(name)

    def __getitem__(self, k):
        return self.ap[k]


def bc_rows(dram_ap_2d, row, ncols, nparts=128):
    return bass.AP(dram_ap_2d.tensor, row * ncols, [[0, nparts], [1, ncols]])


def build(dbg=()):
    nc = bass.Bass("TRN2", target_bir_lowering=False)
    din = lambda n, s, dt=F32: nc.dram_tensor(n, s, dt, kind="ExternalInput").ap()
    dout = lambda n, s: nc.dram_tensor(n, s, F32, kind="ExternalOutput").ap()
    x_d = din("x", [NT * 128, D])
    sret_d = din("sret", [16, 4, 128, 128])
    sC_d = din("sC", [16, 4, 128, 128])
    sn_d = din("sn", [16, 4, 128])
    sm_d = din("sm", [16, 4])
    sconv_d = din("sconv", [32, 2 * DFF])
    w_in_d = din("w_in", [D, INC])
    w_out_d = din("w_out", [D, D])
    w_up_d = din("w_up", [D, 2 * DFF])
    w_down_d = din("w_down", [DFF, D])
    gfm_d = din("gfm", [128, 24])
    gpost_d = din("gpost", [2, D])
    bg_d = din("bg", [1, 8])
    convp_d = din("convp", [128, 44 * 4])
    cst_d = din("consts", [2, 128, NCONST])
    bm_d = din("bm", [128, 2048])
    rot_d = din("rot", [NT, 128, 256])

    y_d = dout("y", [NT * 128, D])
    Sp_d = dout("Sp", [4, 128, 128])
    Ss_d = dout("Ss", [16, 4, 128, 128])
    Cp_d = dout("Cp", [4, 128, 128])
    Cs_d = dout("Cs", [16, 4, 128, 128])
    np_d = dout("np", [4, 128])
    ns_d = dout("ns", [16, 4, 128])
    mp_d = dout("mp", [1, 4])
    ms_d = dout("ms", [16, 4])
    cvp_d = dout("cvp", [2, 2 * DFF])
    cvs_d = dout("cvs", [32, 2 * DFF])
    dbg_d = {n: dout("dbg_" + n, s) for n, s in dbg}

    wout_scr = nc.dram_tensor("wout_scr", [D, D], BF16, kind="Internal").ap()
    wdn_scr = nc.dram_tensor("wdn_scr", [DFF, D], BF16, kind="Internal").ap()
    wup_scr = nc.dram_tensor("wup_scr", [D, 2 * DFF], BF16, kind="Internal").ap()

    with contextlib.ExitStack() as es:
        S = Sched(nc, es)
        PE, ACT, DVE, POOL = S.pe, S.act, S.dve, S.pool
        QS, QP = S.q_sp, S.q_pool

        def sb(stack, name, shape, dt=F32):
            return B(stack.enter_context(nc.sbuf_tensor("sb_" + name, shape, dt)), name)

        def v_op(fn, r=(), w=()):
            return S.op(DVE, fn, r, w)

        def a_op(fn, r=(), w=()):
            return S.op(ACT, fn, r, w)

        def p_op(fn, r=(), w=()):
            return S.op(POOL, fn, r, w)

        def mm(out, lhsT, rhs, start, stop, r=(), w=(), sig=False):
            return S.op(PE, lambda e: e.matmul(out, lhsT=lhsT, rhs=rhs, start=start, stop=stop), r, w, sig)

        def tr(out, in_, ident, r=(), w=(), sig=False):
            return S.op(PE, lambda e: e.transpose(out, in_, ident), r, w, sig)

        def dump(name, ap, tt):
            if name in dbg_d:
                S.dma(QS, dbg_d[name], ap, r=tt)

        banks = [B(es.enter_context(nc.psum_tensor("bank%d" % i, [128, 512], F32)), "bank%d" % i) for i in range(8)]
        ring = {"i": 0, "n": 8, "base": 0, "held": set()}

        def nb(hold=False):
            while True:
                idx = ring["base"] + ring["i"] % ring["n"]
                ring["i"] += 1
                if idx not in ring["held"]:
                    break
            if hold:
                ring["held"].add(idx)
            return banks[idx]

        def rel(bank):
            ring["held"].discard(banks.index(bank))

        def bfv(bank):
            return bank.ap.bitcast(BF16)

        X1 = sb(es, "X1", [128, NT, D])
        x1t = [T("x1_%d" % t) for t in range(NT)]
        identf = sb(es, "identf", [128, 128])
        identb = sb(es, "identb", [128, 128], BF16)
        gfm = sb(es, "gfm", [128, 24])
        gpost = sb(es, "gpost", [128, D])
        nhalf = sb(es, "nhalf", [128, 8])
        eps_t = sb(es, "eps_t", [128, 1])
        ss = sb(es, "ss", [128, 4])
        rs = sb(es, "rs", [128, 4])
        scr_t = [T("wout_scr")]
        wdn_t = [T("wdn_scr%d" % i) for i in range(NJ)]
        wups_t = [[T("wup_scr%d_%d" % (k, hc)) for hc in range(4)] for k in range(8)]
        wup_casts = []

        def issue_wup_casts(n):
            for _ in range(n):
                if wup_casts:
                    kind, k, hc = wup_casts.pop(0)
                    if kind == "d":
                        S.dma(QP, wdn_scr[k * 128:(k + 1) * 128, :], w_down_d[k * 128:(k + 1) * 128, :], w=[wdn_t[k]])
                    else:
                        c0, c1 = hc * 1408, (hc + 1) * 1408
                        S.dma(QP, wup_scr[k * 128:(k + 1) * 128, c0:c1], w_up_d[k * 128:(k + 1) * 128, c0:c1], w=[wups_t[k][hc]])
        jblk = [(0, 6), (6, 12), (12, 17), (17, 22)]
        ident = identf[:, :]

        def rstd_pow(out_b, out_ap, in_b, in_ap, ncol, scale):
            v_op(lambda e: e.tensor_scalar(out=out_ap, in0=in_ap, scalar1=scale, scalar2=EPS, op0=ALU.mult, op1=ALU.add),
                 r=[in_b.t], w=[out_b.t])
            p_op(lambda e: e.tensor_tensor(out=out_ap, in0=out_ap, in1=nhalf[:, 0:ncol], op=ALU.pow),
                 r=[out_b.t, nhalf.t], w=[out_b.t])

        S.dma(QS, identf[:], cst_d[0, :, C_ID:C_ID + 128], w=[identf.t])
        S.dma(QS, gfm[:], gfm_d[:, :], w=[gfm.t])
        S.dma(QS, gpost[:], bc_rows(gpost_d, 0, D), w=[gpost.t])
        v_op(lambda e: e.memset(nhalf[:], -0.5), w=[nhalf.t])
        v_op(lambda e: e.memset(eps_t[:], EPS), w=[eps_t.t])
        v_op(lambda e: e.tensor_copy(identb[:], ident), r=[identf.t], w=[identb.t])

        with contextlib.ExitStack() as sa:
            cst = sb(sa, "cst", [128, NCONST])
            S.dma(QS, cst[:], cst_d[1], w=[cst.t])
            w_in = sb(sa, "w_in_bf", [128, 8, INC], BF16)
            w_in_t = [[T("w_in_%d_%d" % (k, c)) for c in range(3)] for k in range(8)]
            cblk = [(0, 1536), (1536, 3072), (3072, INC)]
            stg = [X1.ap[:, 2 * i:2 * i + 2, :].rearrange("p t c -> p (t c)") for i in range(8)]
            stg_t = [T("stg%d" % i) for i in range(8)]
            npc = 0
            for c in (2, 0, 1):
                c0, c1 = cblk[c]
                for k in range(8):
                    if k % 2 == 0:
                        S.dma(QP, w_in[:, k, c0:c1], w_in_d[k * 128:(k + 1) * 128, c0:c1], w=[w_in_t[k][c]])
                    else:
                        i = npc % 8
                        S.dma(QS, stg[i][:, 0:c1 - c0], w_in_d[k * 128:(k + 1) * 128, c0:c1], w=[stg_t[i]])
                        if npc % 2 == 0:
                            v_op(lambda e: e.tensor_copy(w_in[:, k, c0:c1], stg[i][:, 0:c1 - c0]), r=[stg_t[i]], w=[w_in_t[k][c]])
                        else:
                            a_op(lambda e: e.copy(w_in[:, k, c0:c1], stg[i][:, 0:c1 - c0]), r=[stg_t[i]], w=[w_in_t[k][c]])
                        npc += 1
            S.dma(QP, wout_scr[:, :], w_out_d[:, :], w=[scr_t[0]])
            wo_ring = [sb(sa, "wo%d" % i, [128, D], BF16) for i in range(3)]
            wo_n = [0]
            wo_q = []
            bgb = sb(sa, "bgb", [128, 8])
            S.dma(QS, bgb[:], bc_rows(bg_d, 0, 8), w=[bgb.t])
            rot = [sb(sa, "rot%d" % i, [128, 256]) for i in range(2)]
            h_bf = sb(sa, "h_bf", [128, D], BF16)
            hm_bf = sb(sa, "mix_bf", [128, D], BF16)
            hTs = [sb(sa, "hTf%d" % i, [128, 8, 128], BF16) for i in range(2)]
            mixT = sb(sa, "mixT", [128, 8, 128], BF16)
            qkf = sb(sa, "qkf", [128, 3, 512], BF16)
            v_m = sb(sa, "v_m", [128, 4, 128], BF16)
            gtmp = sb(sa, "gtmp", [128, 512])
            sT = [sb(sa, "sT%d" % i, [128, 4, 128], BF16) for i in range(2)]
            hout = sb(sa, "hout", [128, 512])
            gtmpb = hout
            rtmp = B(gtmp.ap[:, 0:256].rearrange("p (h d) -> p h d", h=4), "rtmp")
            rtmp2 = B(gtmp.ap[:, 256:512].rearrange("p (h d) -> p h d", h=4), "rtmp2")
            rtmp.t = gtmp.t
            rtmp2.t = gtmp.t
            g8 = sb(sa, "g8", [128, 8])
            lt = sb(sa, "lt", [128, 4])
            Bl = [sb(sa, "Bl%d" % i, [128, 4]) for i in range(2)]
            Av = sb(sa, "Av", [128, 4])
            AT = sb(sa, "AT", [4, 128])
            ulast = [sb(sa, "ulast%d" % i, [4, 16]) for i in range(2)]
            ultx = sb(sa, "ultx", [4, 128])
            UL = [sb(sa, "UL%d" % i, [128, 4]) for i in range(2)]
            d12 = sb(sa, "d12", [128, 12])
            st6 = sb(sa, "st6", [128, 8, 6])
            mv = sb(sa, "mv", [128, 8, 2])
            gsc = sb(sa, "gsc", [128, 8])
            gbi = sb(sa, "gbi", [128, 8])
            den = sb(sa, "den", [128, 4])
            rden = sb(sa, "rden", [128, 4])
            n16buf = sb(sa, "n16buf", [128, 512])
            n16 = B(n16buf.ap[0:16, :].rearrange("p (h d) -> p h d", h=4), "n16")
            n16.t = n16buf.t
            wo3 = B(n16buf.ap.bitcast(BF16), "wo3")
            wo3.t = n16buf.t
            ssb = sb(sa, "ssb", [128, 4])
            rsb = sb(sa, "rsb", [128, 4])

            def mkset(stack, i):
                d = {}
                d["km"] = sb(stack, "km%d" % i, [128, 512], BF16)
                d["kz"] = sb(stack, "kz%d" % i, [128, 4, 128], BF16)
                d["v_r"] = sb(stack, "v_r%d" % i, [128, 4, 128], BF16)
                d["vE"] = sb(stack, "vE%d" % i, [128, 4, 128], BF16)
                d["gate"] = sb(stack, "gate%d" % i, [128, D])
                d["qkT"] = sb(stack, "qkT%d" % i, [128, 16, 128], BF16)
                d["e12"] = sb(stack, "e12_%d" % i, [128, 12])
                d["ea_bf"] = sb(stack, "ea_bf%d" % i, [128, 4], BF16)
                d["mrow"] = sb(stack, "mrow%d" % i, [128, 4])
                return d

            sets = [mkset(sa, 0)]

            def wo_prefetch(n):
                for _ in range(n):
                    if not wo_ring:
                        return
                    c = wo_n[0]
                    b = wo_ring.pop(0)
                    wo_n[0] += 1
                    k = c % 8
                    S.dma(QS, b[:, :], wout_scr[k * 128:(k + 1) * 128, :], r=[scr_t[0]], w=[b.t])
                    wo_q.append(b)

            def load_wo(k):
                if not wo_q:
                    wo_prefetch(1)
                b = wo_q.pop(0)
                return b

            def nstage(t):
                xs = X1[:, t, :]
                hmT = hTs[t % 2]
                a_op(lambda e: e.activation(out=h_bf[:], in_=xs, func=AF.Square, accum_out=ss[:, 0:1]),
                     r=[x1t[t]], w=[h_bf.t, ss.t])
                rstd_pow(rs, rs[:, 0:1], ss, ss[:, 0:1], 1, 1.0 / D)
                a_op(lambda e: e.activation(out=h_bf[:], in_=xs, func=AF.Identity, scale=rs[:, 0:1]),
                     r=[x1t[t], rs.t], w=[h_bf.t])

            def nstage2(t):
                hmT = hTs[t % 2]
                pt = nb()
                ptb = bfv(pt)
                for k in range(8):
                    tr(ptb[:, k * 128:(k + 1) * 128], h_bf[:, k * 128:(k + 1) * 128], identb[:],
                       r=[h_bf.t, identb.t], w=[pt.t], sig=(k == 7))
                v_op(lambda e: e.tensor_tensor(out=hmT[:], in0=ptb[:, :].rearrange("p (k n) -> p k n", k=8),
                                               in1=gfm[:, 0:8].unsqueeze(2).broadcast_to([128, 8, 128]), op=ALU.mult),
                     r=[pt.t, gfm.t], w=[hmT.t])

            def front(t, cur, ctx):
                sample = (t == ST)
                rt = rot[t % 2]
                hmT = hTs[t % 2]
                km, kz, v_r, vE, gate, qkT, e12, ea_bf, mrow = (cur[n] for n in ("km", "kz", "v_r", "vE", "gate", "qkT", "e12", "ea_bf", "mrow"))
                S.dma(QS, rt[:], rot_d[t], w=[rt.t])

                def proj(g, ncols=512):
                    pb = nb()
                    c0 = g * 512
                    for k in range(8):
                        mm(pb[:, 0:ncols], hmT[:, k, :], w_in[:, k, c0:c0 + ncols], k == 0, k == 7,
                           r=[hmT.t, w_in_t[k][g // 3]], w=[pb.t], sig=(k == 7))
                    return pb

                def rotary(pb, dst, ci):
                    pv = pb[:, :].rearrange("p (h d) -> p h d", h=4)
                    x1_, x2_ = pv[:, :, 0:64], pv[:, :, 64:128]
                    cc = rt[:, ci * 64:(ci + 1) * 64].unsqueeze(1).broadcast_to([128, 4, 64])
                    sn_ = rt[:, (ci + 1) * 64:(ci + 2) * 64].unsqueeze(1).broadcast_to([128, 4, 64])
                    dv = dst.rearrange("p (h d) -> p h d", h=4)
                    v_op(lambda e: e.tensor_tensor(out=rtmp[:], in0=x1_, in1=cc, op=ALU.mult), r=[pb.t, rt.t], w=[rtmp.t])
                    v_op(lambda e: e.tensor_tensor(out=rtmp2[:], in0=x2_, in1=sn_, op=ALU.mult), r=[pb.t, rt.t], w=[rtmp2.t])
                    p_op(lambda e: e.tensor_tensor(out=dv[:, :, 0:64], in0=rtmp[:], in1=rtmp2[:], op=ALU.subtract),
                         r=[rtmp.t, rtmp2.t], w=[qkf.t])
                    v_op(lambda e: e.tensor_tensor(out=rtmp[:], in0=x1_, in1=sn_, op=ALU.mult), r=[pb.t, rt.t], w=[rtmp.t])
                    v_op(lambda e: e.tensor_tensor(out=rtmp2[:], in0=x2_, in1=cc, op=ALU.mult), r=[pb.t, rt.t], w=[rtmp2.t])
                    p_op(lambda e: e.tensor_tensor(out=dv[:, :, 64:128], in0=rtmp[:], in1=rtmp2[:], op=ALU.add),
                         r=[rtmp.t, rtmp2.t], w=[qkf.t])

                def sigmoid_from(pb):
                    a_op(lambda e: e.activation(out=gtmp[:], in_=pb[:, :], func=AF.Exp, scale=-1.0), r=[pb.t], w=[gtmp.t])
                    a_op(lambda e: e.activation(out=gtmp[:], in_=gtmp[:], func=AF.Ln, bias=1.0), r=[gtmp.t], w=[gtmp.t])
                    a_op(lambda e: e.activation(out=gtmp[:], in_=gtmp[:], func=AF.Exp, scale=-1.0), r=[gtmp.t], w=[gtmp.t])

                Bc, Bp = Bl[t % 2], Bl[(t + 1) % 2]
                ULc, ULp = UL[t % 2], UL[(t + 1) % 2]
                ulc, ulp = ulast[t % 2], ulast[(t + 1) % 2]
                first = sample or t == 0
                pb = proj(8, 8)
                v_op(lambda e: e.tensor_tensor(out=g8[:], in0=pb[:, 0:8], in1=bgb[:], op=ALU.add), r=[pb.t, bgb.t], w=[g8.t])
                a_op(lambda e: e.activation(out=lt[:], in_=g8[:, 4:8], func=AF.Exp, scale=-1.0), r=[g8.t], w=[lt.t])
                a_op(lambda e: e.activation(out=lt[:], in_=lt[:], func=AF.Ln, bias=1.0), r=[lt.t], w=[lt.t])
                yield
                pb = proj(0)
                rotary(pb, qkf[:, 0, :], 0)
                yield
                pb = proj(1)
                rotary(pb, qkf[:, 1, :], 2)
                p_op(lambda e: e.tensor_tensor(out=kz[:], in0=qkf[:, 1, :].rearrange("p (h d) -> p h d", h=4),
                                               in1=cst[:, C_Z:C_Z + 4].unsqueeze(2).broadcast_to([128, 4, 128]), op=ALU.mult),
                     r=[qkf.t, cst.t], w=[kz.t])
                yield
                pb = proj(2)
                a_op(lambda e: e.copy(v_r[:].rearrange("p h d -> p (h d)"), pb[:, :]), r=[pb.t], w=[v_r.t])
                pm = nb()
                mm(pm[:, 0:4], cst[:, C_MT:C_MT + 128], lt[:], True, first, r=[cst.t, lt.t], w=[pm.t], sig=first)
                if not first:
                    mm(pm[:, 0:4], cst[:, C_SEL:C_SEL + 128], Bp[:], False, True, r=[cst.t, Bp.t], w=[pm.t], sig=True)
                v_op(lambda e: e.tensor_copy(Bc[:], pm[:, 0:4]), r=[pm.t], w=[Bc.t])
                v_op(lambda e: e.tensor_tensor(out=Av[:], in0=g8[:, 0:4], in1=Bc[:], op=ALU.add), r=[g8.t, Bc.t], w=[Av.t])
                yield
                pb = proj(3)
                sigmoid_from(pb)
                v_op(lambda e: e.tensor_tensor(out=gate[:, 0:512], in0=pb[:, :], in1=gtmp[:], op=ALU.mult),
                     r=[pb.t, gtmp.t], w=[gate.t])
                yield
                pt = nb()
                ptb = bfv(pt)
                for i in range(8):
                    tr(ptb[:, i * 128:(i + 1) * 128], qkf[:, i // 4, (i % 4) * 128:(i % 4 + 1) * 128], identb[:],
                       r=[qkf.t, identb.t], w=[pt.t], sig=(i == 7))
                v_op(lambda e: e.tensor_tensor(out=qkT[:, 0:4, :], in0=ptb[:, 0:512].rearrange("p (h n) -> p h n", h=4),
                                               in1=cst[:, C_XI:C_XI + 512].rearrange("p (h n) -> p h n", h=4), op=ALU.mult),
                     r=[pt.t, cst.t], w=[qkT.t])
                a_op(lambda e: e.copy(qkT[:, 4:8, :], ptb[:, 512:1024].rearrange("p (h n) -> p h n", h=4)),
                     r=[pt.t], w=[qkT.t])
                pm = nb()
                tr(pm[0:4, 0:128], Av[:], ident, r=[Av.t, identf.t], w=[pm.t], sig=True)
                v_op(lambda e: e.tensor_copy(AT[:], pm[0:4, 0:128]), r=[pm.t], w=[AT.t])
                if sample:
                    v_op(lambda e: e.tensor_reduce(out=ulc[:, :], in_=AT[:, :].rearrange("p (j i) -> p j i", i=8),
                                                   axis=AXX, op=ALU.max), r=[AT.t], w=[ulc.t])
                    v_op(lambda e: e.tensor_tensor(out=ulc[:, :], in0=ulc[:, :], in1=ctx["m0T"][:, :], op=ALU.max),
                         r=[ulc.t, ctx["m0T"].t], w=[ulc.t])
                    v_op(lambda e: e.tensor_copy(ultx[:, :].rearrange("p (j i) -> p j i", i=8),
                                                 ulc[:, :].unsqueeze(2).broadcast_to([4, 16, 8])), r=[ulc.t], w=[ultx.t])
                else:
                    v_op(lambda e: e.tensor_reduce(out=ulc[:, 0:1], in_=AT[:, :], axis=AXX, op=ALU.max),
                         r=[AT.t], w=[ulc.t])
                    if t > 0:
                        v_op(lambda e: e.tensor_tensor(out=ulc[:, 0:1], in0=ulc[:, 0:1], in1=ulp[:, 0:1], op=ALU.max),
                             r=[ulc.t, ulp.t], w=[ulc.t])
                    v_op(lambda e: e.tensor_copy(ultx[:, :], ulc[:, 0:1].broadcast_to([4, 128])), r=[ulc.t], w=[ultx.t])
                yield
                pb = proj(4)
                a_op(lambda e: e.copy(qkf[:, 2, :], pb[:, :]), r=[pb.t], w=[qkf.t])
                yield
                pb = proj(5)
                a_op(lambda e: e.mul(km[:], pb[:, :], SCALE), r=[pb.t], w=[km.t])
                pm = nb()
                tr(pm[:, 0:4], ultx[:, :], identf[0:4, 0:4], r=[ultx.t, identf.t], w=[pm.t], sig=True)
                v_op(lambda e: e.tensor_copy(ULc[:], pm[:, 0:4]), r=[pm.t], w=[ULc.t])
                ulprev = ctx["m0x"] if sample else ULp
                v_op(lambda e: e.tensor_tensor(out=d12[:, 0:4], in0=Av[:], in1=ULc[:], op=ALU.subtract), r=[Av.t, ULc.t], w=[d12.t])
                v_op(lambda e: e.tensor_tensor(out=d12[:, 4:8], in0=Bc[:], in1=ULc[:], op=ALU.subtract), r=[Bc.t, ULc.t], w=[d12.t])
                if sample or t > 0:
                    v_op(lambda e: e.tensor_tensor(out=d12[:, 8:12], in0=ulprev[:], in1=ULc[:], op=ALU.subtract),
                         r=[ulprev.t, ULc.t], w=[d12.t])
                else:
                    v_op(lambda e: e.memset(d12[:, 8:12], -100.0), w=[d12.t])
                a_op(lambda e: e.activation(out=e12[:], in_=d12[:], func=AF.Exp), r=[d12.t], w=[e12.t])
                v_op(lambda e: e.tensor_copy(ea_bf[:], e12[:, 0:4]), r=[e12.t], w=[ea_bf.t])
                v_op(lambda e: e.tensor_tensor(out=mrow[:], in0=ULc[:], in1=Bc[:], op=ALU.subtract), r=[ULc.t, Bc.t], w=[mrow.t])
                yield
                pb = proj(6)
                a_op(lambda e: e.copy(v_m[:].rearrange("p h d -> p (h d)"), pb[:, :]), r=[pb.t], w=[v_m.t])
                p_op(lambda e: e.tensor_tensor(out=vE[:], in0=v_m[:], in1=e12[:, 0:4].unsqueeze(2).broadcast_to([128, 4, 128]), op=ALU.mult),
                     r=[v_m.t, e12.t], w=[vE.t])
                yield
                pb = proj(7)
                sigmoid_from(pb)
                a_op(lambda e: e.copy(gate[:, 512:1024], gtmp[:]), r=[gtmp.t], w=[gate.t])
                yield
                pt = nb()
                ptb = bfv(pt)
                for i in range(8):
                    src = qkf[:, 2, i * 128:(i + 1) * 128] if i < 4 else km[:, (i - 4) * 128:(i - 3) * 128]
                    tr(ptb[:, i * 128:(i + 1) * 128], src, identb[:], r=[qkf.t, km.t, identb.t], w=[pt.t], sig=(i == 7))
                a_op(lambda e: e.copy(qkT[:, 8:16, :], ptb[:, :].rearrange("p (h n) -> p h n", h=8)),
                     r=[pt.t], w=[qkT.t])
                yield

            def back(t, cur, heads_gen):
                xs = X1[:, t, :]
                gate, qkT, e12 = cur["gate"], cur["qkT"], cur["e12"]
                for grp in range(2):
                    ps_ = nb()
                    qo, ko = (0, 4) if grp == 0 else (8, 12)
                    for h in range(4):
                        mm(ps_[:, h * 128:(h + 1) * 128], qkT[:, ko + h, :], qkT[:, qo + h, :], True, True,
                           r=[qkT.t], w=[ps_.t], sig=(h == 3))
                    if grp == 0:
                        v_op(lambda e: e.tensor_tensor(out=sT[0][:].rearrange("p h n -> p (h n)"), in0=ps_[:, :],
                                                       in1=cst[:, C_R:C_R + 512], op=ALU.mult), r=[ps_.t, cst.t], w=[sT[0].t])
                    else:
                        v_op(lambda e: e.tensor_tensor(out=sT[1][:], in0=ps_[:, :].rearrange("p (h n) -> p h n", h=4),
                                                       in1=cst[:, C_MT:C_MT + 128].unsqueeze(1).broadcast_to([128, 4, 128]), op=ALU.mult),
                             r=[ps_.t, cst.t], w=[sT[1].t])
                    yield
                res = {}
                for _ in heads_gen(t, cur, res):
                    yield
                po = [res["po0"], res["po1"]]
                pdd = res["pdd"]
                for h in range(4):
                    v_op(lambda e: e.bn_stats(st6[:, h, :], po[0][:, h * 128:(h + 1) * 128]), r=[po[0].t], w=[st6.t])
                    v_op(lambda e: e.bn_aggr(mv[:, h, :], st6[:, h, :]), r=[st6.t], w=[mv.t])
                v_op(lambda e: e.tensor_scalar(out=den[:], in0=pdd[:, 0:4], scalar1=-1.0, scalar2=None, op0=ALU.mult), r=[pdd.t], w=[den.t])
                v_op(lambda e: e.tensor_tensor(out=den[:], in0=den[:], in1=pdd[:, 0:4], op=ALU.max), r=[den.t, pdd.t], w=[den.t])
                v_op(lambda e: e.tensor_tensor(out=den[:], in0=den[:], in1=e12[:, 4:8], op=ALU.max), r=[den.t, e12.t], w=[den.t])
                v_op(lambda e: e.reciprocal(rden[:], den[:]), r=[den.t], w=[rden.t])
                if pdd in banks:
                    rel(pdd)
                yield
                for h in range(4):
                    v_op(lambda e: e.bn_stats(st6[:, 4 + h, :], po[1][:, h * 128:(h + 1) * 128]), r=[po[1].t], w=[st6.t])
                    v_op(lambda e: e.bn_aggr(mv[:, 4 + h, :], st6[:, 4 + h, :]), r=[st6.t], w=[mv.t])
                v_op(lambda e: e.tensor_tensor(out=mv[:, 4:8, 0], in0=mv[:, 4:8, 0], in1=rden[:], op=ALU.mult), r=[mv.t, rden.t], w=[mv.t])
                v_op(lambda e: e.tensor_tensor(out=mv[:, 4:8, 1], in0=mv[:, 4:8, 1], in1=rden[:], op=ALU.mult), r=[mv.t, rden.t], w=[mv.t])
                v_op(lambda e: e.tensor_tensor(out=mv[:, 4:8, 1], in0=mv[:, 4:8, 1], in1=rden[:], op=ALU.mult), r=[mv.t, rden.t], w=[mv.t])
                yield
                a_op(lambda e: e.activation(out=gsc[:], in_=mv[:, :, 1], func=AF.Ln, bias=eps_t[:, 0:1]), r=[mv.t, eps_t.t], w=[gsc.t])
                a_op(lambda e: e.activation(out=gsc[:], in_=gsc[:], func=AF.Exp, scale=-0.5), r=[gsc.t], w=[gsc.t])
                v_op(lambda e: e.scalar_tensor_tensor(out=gbi[:], in0=mv[:, :, 0], scalar=-1.0, in1=gsc[:], op0=ALU.mult, op1=ALU.mult),
                     r=[mv.t, gsc.t], w=[gbi.t])
                v_op(lambda e: e.tensor_tensor(out=gsc[:, 4:8], in0=gsc[:, 4:8], in1=rden[:], op=ALU.mult), r=[gsc.t, rden.t], w=[gsc.t])
                for grp in range(2):
                    for h in range(4):
                        a = grp * 4 + h
                        if grp == 0:
                            a_op(lambda e: e.activation(out=hout[:, h * 128:(h + 1) * 128], in_=po[grp][:, h * 128:(h + 1) * 128],
                                                        func=AF.Identity, scale=gsc[:, a:a + 1], bias=gbi[:, a:a + 1]),
                                 r=[po[grp].t, gsc.t, gbi.t], w=[hout.t])
                        else:
                            v_op(lambda e: e.tensor_scalar(out=hout[:, h * 128:(h + 1) * 128], in0=po[grp][:, h * 128:(h + 1) * 128],
                                                           scalar1=gsc[:, a:a + 1], scalar2=gbi[:, a:a + 1], op0=ALU.mult, op1=ALU.add),
                                 r=[po[grp].t, gsc.t, gbi.t], w=[hout.t])
                    v_op(lambda e: e.tensor_tensor(out=hm_bf[:, grp * 512:(grp + 1) * 512], in0=hout[:], in1=gate[:, grp * 512:(grp + 1) * 512], op=ALU.mult),
                         r=[hout.t, gate.t], w=[hm_bf.t])
                    rel(po[grp])
                    yield
                pt = nb()
                ptb = bfv(pt)
                for k in range(8):
                    tr(ptb[:, k * 128:(k + 1) * 128], hm_bf[:, k * 128:(k + 1) * 128], identb[:], r=[hm_bf.t, identb.t], w=[pt.t], sig=(k == 7))
                v_op(lambda e: e.tensor_tensor(out=mixT[:], in0=ptb[:, :].rearrange("p (k n) -> p k n", k=8),
                                               in1=gfm[:, 8:16].unsqueeze(2).broadcast_to([128, 8, 128]), op=ALU.mult),
                     r=[pt.t, gfm.t], w=[mixT.t])
                yield
                pa, pb2 = nb(True), nb(True)
                for k in range(8):
                    wb = load_wo(k)
                    mm(pa[:, :], mixT[:, k, :], wb[:, 0:512], k == 0, k == 7, r=[mixT.t, wb.t], w=[pa.t], sig=False)
                    mm(pb2[:, :], mixT[:, k, :], wb[:, 512:1024], k == 0, k == 7, r=[mixT.t, wb.t], w=[pb2.t], sig=True)
                    wo_ring.append(wb)
                    wo_prefetch(1)
                    if k == 3:
                        yield
                a_op(lambda e: e.activation(out=gtmpb[:], in_=pa[:, :], func=AF.Square, accum_out=ssb[:, 1:2]), r=[pa.t], w=[gtmpb.t, ssb.t])
                a_op(lambda e: e.activation(out=gtmpb[:], in_=pb2[:, :], func=AF.Square, accum_out=ssb[:, 2:3]), r=[pb2.t], w=[gtmpb.t, ssb.t])
                v_op(lambda e: e.tensor_tensor(out=ssb[:, 1:2], in0=ssb[:, 1:2], in1=ssb[:, 2:3], op=ALU.add), r=[ssb.t], w=[ssb.t])
                rstd_pow(rsb, rsb[:, 1:2], ssb, ssb[:, 1:2], 1, 1.0 / D)
                yield
                for i, pbx in enumerate((pa, pb2)):
                    v_op(lambda e: e.scalar_tensor_tensor(out=gtmpb[:], in0=pbx[:, :], scalar=rsb[:, 1:2],
                                                          in1=gpost[:, i * 512:(i + 1) * 512], op0=ALU.mult, op1=ALU.mult),
                         r=[pbx.t, rsb.t, gpost.t], w=[gtmpb.t])
                    p_op(lambda e: e.tensor_tensor(out=xs[:, i * 512:(i + 1) * 512], in0=xs[:, i * 512:(i + 1) * 512], in1=gtmpb[:], op=ALU.add),
                         r=[x1t[t], gtmpb.t], w=[x1t[t]])
                    rel(pbx)
                if "tail" in res:
                    res["tail"]()
                yield

            SCHED = "NbpbpbbbpbpbbpppppbnpbpBb"

            def merged(gb, gp, nt):
                gens = {"b": gb, "p": gp}
                cnt = {"b": 0, "p": 0}
                for c in SCHED:
                    cc_ = c.lower()
                    if cc_ in cnt:
                        Sched.label = "%s%d(nt=%s)" % (cc_, cnt[cc_], nt)
                        cnt[cc_] += 1
                    else:
                        Sched.label = c + "(nt=%s)" % nt
                    if c == "N":
                        if nt is not None:
                            nstage(nt)
                    elif c == "n":
                        if nt is not None:
                            nstage2(nt)
                    elif c == "B":
                        c = "b"
                        if gens[c] is not None:
                            try:
                                next(gens[c])
                            except StopIteration:
                                gens[c] = None
                    elif gens[c] is not None:
                        try:
                            next(gens[c])
                        except StopIteration:
                            gens[c] = None
                for g in gens.values():
                    if g is not None:
                        for _ in g:
                            pass

            def run_seq(*gens):
                for g in gens:
                    for _ in g:
                        pass

            def interleave(ga, gb):
                gens = [g for g in (ga, gb) if g is not None]
                while gens:
                    for g in list(gens):
                        try:
                            next(g)
                        except StopIteration:
                            gens.remove(g)

            with contextlib.ExitStack() as s2:
                bm = sb(s2, "bm", [128, 16, 128], BF16)
                S.dma(QP, bm[:].rearrange("p j n -> p (j n)"), bm_d[:, :], w=[bm.t])
                qpad = sb(s2, "qpad", [128, 16, 128], BF16)
                vpad = sb(s2, "vpad", [128, 16, 128], BF16)
                Sbf = [sb(s2, "Sbf%d" % i, [128, 4, 128], BF16) for i in range(2)]
                m0T = sb(s2, "m0T", [4, 16])
                m0x = sb(s2, "m0x", [128, 4])
                nx = sb(s2, "nx", [128, 4, 128])
                nq = sb(s2, "nq", [128, 4])
                k16 = sb(s2, "k16", [16, 4])
                wlm = sb(s2, "wlm", [128, 4, 16], BF16)
                ddx = sb(s2, "ddx", [128, 4])
                kd = sb(s2, "kd", [128, 4, 16])
                kbc = sb(s2, "kbc", [128, 4, 16])
                ones_ = sb(s2, "ones", [128, 128])
                pdsb = sb(s2, "pdsb", [128, 4])
                stage = X1.ap[:, 0:16, :].rearrange("p t c -> p (t c)")
                Sin4 = stage[:, 0:8192].rearrange("p (j h v) -> p j h v", h=4, v=128)
                Cin4 = stage[:, 8192:16384].rearrange("p (j h v) -> p j h v", h=4, v=128)
                Sin_t = [T("Sin%d" % j) for j in range(16)]
                Cin_t = [T("Cin%d" % j) for j in range(16)]
                S.dma(QS, X1[:, ST, :], x_d[ST * 128:(ST + 1) * 128, :], w=[x1t[ST]])
                with nc.allow_non_contiguous_dma(reason="tiny transposed state loads"):
                    S.dma(QS, m0T[:], sm_d.rearrange("j h -> h j"), w=[m0T.t])
                S.dma(QS, m0x[:], bass.AP(sm_d.tensor, 0, [[4, 16], [0, 8], [1, 4]]), w=[m0x.t])
                S.dma(QS, nx[:].rearrange("p h d -> p (h d)"), bass.AP(sn_d.tensor, 0, [[512, 16], [0, 8], [1, 512]]), w=[nx.t])
                S.dma(QS, n16[:], sn_d[:, :, :], w=[n16.t])
                for j in range(16):
                    S.dma(QS, Sin4[:, j, :, :], sret_d[j].rearrange("h d v -> d h v"), r=[w_in_t[7][1]] if j == 0 else [],
                          w=[Sin_t[j], stg_t[(j // 2) // 2]])
                    S.dma(QS, Cin4[:, j, :, :], sC_d[j].rearrange("h v d -> v h d"), w=[Cin_t[j], stg_t[(8 + j // 2) // 2]])
                v_op(lambda e: e.memset(ones_[:], 1.0), w=[ones_.t])
                sbn = [0]
                cd8 = [g ** 8 for g in GAM]

                def heads_sample(t, cur, res):
                    km, kz, v_r, vE, qkT, e12, ea_bf, mrow = (cur[n] for n in ("km", "kz", "v_r", "vE", "qkT", "e12", "ea_bf", "mrow"))
                    v_op(lambda e: e.tensor_tensor(out=kd[:], in0=e12[:, 8:12].unsqueeze(2).broadcast_to([128, 4, 16]),
                                                   in1=cst[:, C_FI:C_FI + 16].unsqueeze(1).broadcast_to([128, 4, 16]), op=ALU.mult),
                         r=[e12.t, cst.t], w=[kd.t])
                    pk = nb()
                    mm(pk[:, 0:64], ones_[:], kd[:].rearrange("p h j -> p (h j)"), True, True, r=[ones_.t, kd.t], w=[pk.t], sig=True)
                    v_op(lambda e: e.tensor_copy(kbc[:].rearrange("p h j -> p (h j)"), pk[:, 0:64]), r=[pk.t], w=[kbc.t])
                    pk = nb()
                    mm(pk[0:16, 0:4], cst[:, C_FI:C_FI + 16], e12[:, 8:12], True, True, r=[cst.t, e12.t], w=[pk.t], sig=True)
                    v_op(lambda e: e.tensor_copy(k16[:], pk[0:16, 0:4]), r=[pk.t], w=[k16.t])
                    for h in range(4):
                        v_op(lambda e: e.scalar_tensor_tensor(out=hout[:, 0:128], in0=qkf[:, 2, h * 128:(h + 1) * 128], scalar=1.0, in1=nx[:, h, :],
                                                              op0=ALU.mult, op1=ALU.mult, accum_out=nq[:, h:h + 1]),
                             r=[qkf.t, nx.t], w=[hout.t, nq.t])
                    pos = [nb(True), nb(True)]
                    pdd = nb(True)
                    for grp in range(2):
                        qo = 0 if grp == 0 else 8
                        vsrc = v_r if grp == 0 else vE
                        for h in range(4):
                            v_op(lambda e: e.tensor_tensor(out=qpad[:], in0=qkT[:, qo + h, :].unsqueeze(1).broadcast_to([128, 16, 128]),
                                                           in1=bm[:], op=ALU.mult), r=[qkT.t, bm.t], w=[qpad.t])
                            o_ap = pos[grp][:, h * 128:(h + 1) * 128]
                            mm(o_ap, sT[grp][:, h, :], vsrc[:, h, :], True, False, r=[sT[grp].t, vsrc.t], w=[pos[grp].t])
                            for j4 in range(4):
                                sb_ = Sbf[sbn[0] % 2]
                                sbn[0] += 1
                                jt = [j4 * 4 + i for i in range(4)]
                                if grp == 0:
                                    a_op(lambda e: e.copy(sb_[:], Sin4[:, j4 * 4:(j4 + 1) * 4, h, :]),
                                         r=[Sin_t[j] for j in jt], w=[sb_.t])
                                else:
                                    pc = nb()
                                    for i, j in enumerate(jt):
                                        tr(pc[:, i * 128:(i + 1) * 128], Cin4[:, j, h, :], ident,
                                           r=[Cin_t[j], identf.t], w=[pc.t], sig=(i == 3))
                                    v_op(lambda e: e.tensor_tensor(out=sb_[:], in0=pc[:, :].rearrange("p (j v) -> p j v", j=4),
                                                                   in1=kbc[:, h, j4 * 4:(j4 + 1) * 4].unsqueeze(2).broadcast_to([128, 4, 128]), op=ALU.mult),
                                         r=[pc.t, kbc.t], w=[sb_.t])
                                for i, j in enumerate(jt):
                                    last = (j == 15)
                                    mm(o_ap, qpad[:, j, :], sb_[:, i, :], False, last, r=[qpad.t, sb_.t], w=[pos[grp].t], sig=(last or i == 3))
                    for h in range(4):
                        mm(pdd[:, h:h + 1], sT[1][:, h, :], ea_bf[:, h:h + 1], True, True, r=[sT[1].t, ea_bf.t], w=[pdd.t], sig=(h == 3))
                    v_op(lambda e: e.tensor_tensor(out=ddx[:], in0=nq[:], in1=e12[:, 8:12], op=ALU.mult), r=[nq.t, e12.t], w=[ddx.t])
                    v_op(lambda e: e.tensor_tensor(out=pdsb[:], in0=pdd[:, 0:4], in1=ddx[:], op=ALU.add), r=[pdd.t, ddx.t], w=[pdsb.t])
                    rel(pdd)
                    for grp in range(2):
                        vsrc = v_r if grp == 0 else vE
                        for h in range(4):
                            p_op(lambda e: e.tensor_tensor(out=vpad[:], in0=vsrc[:, h, :].unsqueeze(1).broadcast_to([128, 16, 128]),
                                                           in1=cst[:, C_RM:C_RM + 16].unsqueeze(2).broadcast_to([128, 16, 128]), op=ALU.mult),
                                 r=[vsrc.t, cst.t], w=[vpad.t])
                            for j4 in range(4):
                                pu = nb()
                                jt = [j4 * 4 + i for i in range(4)]
                                if grp == 0:
                                    mm(pu[:, :], kz[:, h, :], vpad[:, j4 * 4:(j4 + 1) * 4, :].rearrange("p j v -> p (j v)"), True, True,
                                       r=[kz.t, vpad.t], w=[pu.t], sig=True)
                                    for i, j in enumerate(jt):
                                        v_op(lambda e: e.scalar_tensor_tensor(out=Sin4[:, j, h, :], in0=Sin4[:, j, h, :], scalar=cd8[h],
                                                                              in1=pu[:, i * 128:(i + 1) * 128], op0=ALU.mult, op1=ALU.add),
                                             r=[pu.t, Sin_t[j]], w=[Sin_t[j]])
                                else:
                                    for i, j in enumerate(jt):
                                        mm(pu[:, i * 128:(i + 1) * 128], vpad[:, j, :], km[:, h * 128:(h + 1) * 128], True, True,
                                           r=[vpad.t, km.t], w=[pu.t], sig=(i == 3))
                                    for i, j in enumerate(jt):
                                        v_op(lambda e: e.scalar_tensor_tensor(out=Cin4[:, j, h, :], in0=Cin4[:, j, h, :],
                                                                              scalar=kbc[:, h, j:j + 1],
                                                                              in1=pu[:, i * 128:(i + 1) * 128], op0=ALU.mult, op1=ALU.add),
                                             r=[pu.t, Cin_t[j], kbc.t], w=[Cin_t[j]])
                    v_op(lambda e: e.tensor_tensor(out=wlm[:], in0=e12[:, 0:4].unsqueeze(2).broadcast_to([128, 4, 16]),
                                                   in1=cst[:, C_RM:C_RM + 16].unsqueeze(1).broadcast_to([128, 4, 16]), op=ALU.mult),
                         r=[e12.t, cst.t], w=[wlm.t])
                    pn = nb()
                    for h in range(4):
                        mm(pn[0:16, h * 128:(h + 1) * 128], wlm[:, h, :], km[:, h * 128:(h + 1) * 128], True, True,
                           r=[wlm.t, km.t], w=[pn.t], sig=(h == 3))
                    for h in range(4):
                        v_op(lambda e: e.scalar_tensor_tensor(out=n16[:, h, :], in0=n16[:, h, :], scalar=k16[:, h:h + 1],
                                                              in1=pn[0:16, h * 128:(h + 1) * 128], op0=ALU.mult, op1=ALU.add),
                             r=[n16.t, k16.t, pn.t], w=[n16.t])
                    def stores():
                        S.dma(QS, ns_d[:, :, :], n16[:], r=[n16.t])
                        with nc.allow_non_contiguous_dma(reason="tiny m state rows"):
                            S.dma(QS, ms_d[:, :], mrow.ap[7:128:8, :], r=[mrow.t])
                    deferred.append(stores)
                    res["po0"], res["po1"], res["pdd"] = pos[0], pos[1], pdsb
                    yield

                deferred = []
                wo_prefetch(3)
                nstage(ST)
                nstage2(ST)
                run_seq(front(ST, sets[0], {"m0T": m0T, "m0x": m0x}), back(ST, sets[0], heads_sample))
                for fn in deferred:
                    fn()
                for j in range(NJ):
                    wup_casts.append(("d", j, 0))
                for k in range(8):
                    for hc in range(4):
                        wup_casts.append(("u", k, hc))
                S.barrier(dma=False)

            with contextlib.ExitStack() as s3:
                sets.append(mkset(s3, 1))
                S_f = sb(s3, "S_f", [128, 4, 128])
                S_bf = sb(s3, "S_bf", [128, 4, 128], BF16)
                C_f = sb(s3, "C_f", [128, 4, 128])
                C_bf = sb(s3, "C_bf", [128, 4, 128], BF16)
                n_f = sb(s3, "n_f", [128, 4])
                n_bf = sb(s3, "n_bf", [128, 4], BF16)
                ctr = sb(s3, "ctr", [128, 4, 128])
                ntr = sb(s3, "ntr", [4, 128])
                print("phaseA prompt sbuf remaining", nc.sbuf_bytes_remaining)
                cd = [g ** 128 for g in GAM]
                S.dma(QS, cst[:], cst_d[0], w=[cst.t])
                wo_ring.append(wo3)
                wo_prefetch(1)

                def stage_alias(t):
                    if t < 8:
                        return [Sin_t[2 * t], Sin_t[2 * t + 1]]
                    return [Cin_t[2 * (t - 8)], Cin_t[2 * (t - 8) + 1]]

                def load_x(t):
                    for j in (2 * (t % 8), 2 * (t % 8) + 1):
                        if t < 8:
                            S.dma(QS, Ss_d[j].rearrange("h d v -> d h v"), Sin4[:, j, :, :], r=[Sin_t[j]])
                        else:
                            S.dma(QS, Cs_d[j].rearrange("h v d -> v h d"), Cin4[:, j, :, :], r=[Cin_t[j]])
                    S.dma(QS, X1[:, t, :], x_d[t * 128:(t + 1) * 128, :], w=[x1t[t]] + stage_alias(t))

                def heads_prompt(t, cur, res):
                    km, kz, v_r, vE, qkT, e12, ea_bf = (cur[n] for n in ("km", "kz", "v_r", "vE", "qkT", "e12", "ea_bf"))
                    po0, po1, pdd = nb(True), nb(True), nb(True)
                    res["po0"], res["po1"], res["pdd"] = po0, po1, pdd
                    for h in range(4):
                        o_ap = po0[:, h * 128:(h + 1) * 128]
                        mm(o_ap, sT[0][:, h, :], v_r[:, h, :], True, t == 0, r=[sT[0].t, v_r.t], w=[po0.t], sig=(t == 0 and h == 3))
                        if t > 0:
                            mm(o_ap, qkT[:, h, :], S_bf[:, h, :], False, True, r=[qkT.t, S_bf.t], w=[po0.t], sig=(h == 3))
                    pu = nb()
                    for h in range(4):
                        mm(pu[:, h * 128:(h + 1) * 128], kz[:, h, :], v_r[:, h, :], True, True, r=[kz.t, v_r.t], w=[pu.t], sig=(h == 3))
                    for h in range(4):
                        if t == 0:
                            v_op(lambda e: e.tensor_copy(S_f[:, h, :], pu[:, h * 128:(h + 1) * 128]), r=[pu.t], w=[S_f.t])
                        else:
                            v_op(lambda e: e.scalar_tensor_tensor(out=S_f[:, h, :], in0=S_f[:, h, :], scalar=cd[h],
                                                                  in1=pu[:, h * 128:(h + 1) * 128], op0=ALU.mult, op1=ALU.add),
                                 r=[pu.t, S_f.t], w=[S_f.t])
                    res["tail"] = lambda: a_op(lambda e: e.copy(S_bf[:], S_f[:]), r=[S_f.t], w=[S_bf.t])
                    yield
                    if t > 0:
                        v_op(lambda e: e.tensor_tensor(out=C_f[:], in0=C_f[:], in1=e12[:, 8:12].unsqueeze(2).broadcast_to([128, 4, 128]), op=ALU.mult),
                             r=[C_f.t, e12.t], w=[C_f.t])
                        v_op(lambda e: e.tensor_tensor(out=n_f[:], in0=n_f[:], in1=e12[:, 8:12], op=ALU.mult), r=[n_f.t, e12.t], w=[n_f.t])
                        v_op(lambda e: e.tensor_copy(C_bf[:], C_f[:]), r=[C_f.t], w=[C_bf.t])
                        v_op(lambda e: e.tensor_copy(n_bf[:], n_f[:]), r=[n_f.t], w=[n_bf.t])
                    for h in range(4):
                        o_ap = po1[:, h * 128:(h + 1) * 128]
                        mm(o_ap, sT[1][:, h, :], vE[:, h, :], True, t == 0, r=[sT[1].t, vE.t], w=[po1.t], sig=(t == 0 and h == 3))
                        if t > 0:
                            mm(o_ap, qkT[:, 8 + h, :], C_bf[:, h, :], False, True, r=[qkT.t, C_bf.t], w=[po1.t], sig=(h == 3))
                    for h in range(4):
                        mm(pdd[:, h:h + 1], sT[1][:, h, :], ea_bf[:, h:h + 1], True, t == 0, r=[sT[1].t, ea_bf.t], w=[pdd.t], sig=(t == 0 and h == 3))
                        if t > 0:
                            mm(pdd[:, h:h + 1], qkT[:, 8 + h, :], n_bf[:, h:h + 1], False, True, r=[qkT.t, n_bf.t], w=[pdd.t], sig=(h == 3))
                    pu = nb()
                    for h in range(4):
                        mm(pu[:, h * 128:(h + 1) * 128], km[:, h * 128:(h + 1) * 128], vE[:, h, :], True, True, r=[km.t, vE.t], w=[pu.t], sig=(h == 3))
                    pn = nb()
                    for h in range(4):
                        mm(pn[:, h:h + 1], km[:, h * 128:(h + 1) * 128], ea_bf[:, h:h + 1], True, True, r=[km.t, ea_bf.t], w=[pn.t], sig=(h == 3))
                    if t == 0:
                        v_op(lambda e: e.tensor_copy(C_f[:].rearrange("p h v -> p (h v)"), pu[:, :]), r=[pu.t], w=[C_f.t])
                        v_op(lambda e: e.tensor_copy(n_f[:], pn[:, 0:4]), r=[pn.t], w=[n_f.t])
                    else:
                        v_op(lambda e: e.tensor_tensor(out=C_f[:].rearrange("p h v -> p (h v)"), in0=C_f[:].rearrange("p h v -> p (h v)"), in1=pu[:, :], op=ALU.add),
                             r=[pu.t, C_f.t], w=[C_f.t])
                        v_op(lambda e: e.tensor_tensor(out=n_f[:], in0=n_f[:], in1=pn[:, 0:4], op=ALU.add), r=[pn.t, n_f.t], w=[n_f.t])
                    yield

                load_x(0)
                load_x(1)
                load_x(2)
                nstage(0)
                nstage2(0)
                run_seq(front(0, sets[0], {}))
                nstage(1)
                nstage2(1)
                for t in range(16):
                    issue_wup_casts(4)
                    if t + 3 < 16:
                        load_x(t + 3)
                    gF = front(t + 1, sets[(t + 1) % 2], {}) if t + 1 < 16 else None
                    merged(back(t, sets[t % 2], heads_prompt), gF, t + 2 if t + 2 < 16 else None)
                issue_wup_casts(100)
                mrow = sets[15 % 2]["mrow"]
                S.dma(QS, Sp_d.rearrange("h d v -> d h v"), S_f[:], r=[S_f.t])
                pc = nb()
                for h in range(4):
                    tr(pc[:, h * 128:(h + 1) * 128], C_f[:, h, :], ident, r=[C_f.t, identf.t], w=[pc.t], sig=(h == 3))
                v_op(lambda e: e.tensor_copy(ctr[:].rearrange("p h v -> p (h v)"), pc[:, :]), r=[pc.t], w=[ctr.t])
                S.dma(QS, Cp_d.rearrange("h v d -> v h d"), ctr[:], r=[ctr.t])
                pc = nb()
                tr(pc[0:4, 0:128], n_f[:], ident, r=[n_f.t, identf.t], w=[pc.t], sig=True)
                v_op(lambda e: e.tensor_copy(ntr[:], pc[0:4, 0:128]), r=[pc.t], w=[ntr.t])
                S.dma(QS, np_d[:, :], ntr[:], r=[ntr.t])
                S.dma(QS, mp_d[:, :], mrow[127:128, :], r=[mrow.t])
                S.barrier(dma=True)

        with contextlib.ExitStack() as sbk:
            w_up = sb(sbk, "w_up_bf", [128, 8, 2 * DFF], BF16)
            S.dma(QS, gpost[:], bc_rows(gpost_d, 1, D), w=[gpost.t])
            convp = sb(sbk, "convp", [128, 44, 4])
            S.dma(QS, convp[:].rearrange("p j r -> p (j r)"), convp_d[:, :], w=[convp.t])
            w_up_t = [[[T("w_up_%d_%d_%d" % (gv, b_, k)) for k in range(8)] for b_ in range(4)] for gv in range(2)]
            for b_, (j0, j1) in enumerate(jblk):
                for gv in range(2):
                    c0, c1 = gv * DFF + j0 * 128, gv * DFF + j1 * 128
                    for k in range(8):
                        S.dma(QS, w_up[:, k, c0:c1], wup_scr[k * 128:(k + 1) * 128, c0:c1],
                              r=[wups_t[k][hc] for hc in range(c0 // 1408, (c1 - 1) // 1408 + 1)], w=[w_up_t[gv][b_][k]])
            wd_ring = [sb(sbk, "wd%d" % i, [128, D], BF16) for i in range(3)]
            wd_n = [0]
            h2 = sb(sbk, "h2", [128, D], BF16)
            hbs = [sb(sbk, "h2T%d" % i, [128, 8, 258], BF16) for i in range(2)]
            cg = [sb(sbk, "cg%d" % i, [128, 256]) for i in range(2)]
            cv = [sb(sbk, "cv%d" % i, [128, 256]) for i in range(2)]
            hT = [sb(sbk, "hT%d" % i, [128, 256], BF16) for i in range(3)]
            evs = [sb(sbk, "ev%d" % i, [128, D]) for i in range(2)]
            histn = sb(sbk, "histn", [128, 44, 32])
            hist = sb(sbk, "hist", [128, 44, 32])
            ulp2 = sb(sbk, "ulp2", [128, 44, 2])
            cio = [sb(sbk, "cio%d" % i, [32, 512]) for i in range(2)]
            cio_n = [0]
            ring["base"], ring["n"], ring["i"] = 4, 4, 0
            acc = banks[0:4]

            def load_wd(j):
                b = wd_ring[wd_n[0] % 3]
                wd_n[0] += 1
                S.dma(QS, b[:], wdn_scr[j * 128:(j + 1) * 128, :], r=[wdn_t[j]], w=[b.t])
                return b

            def ffn_norm_T(t, hb, col0):
                xs = X1[:, t, :]
                a_op(lambda e: e.activation(out=h2[:], in_=xs, func=AF.Square, accum_out=ss[:, 0:1]), r=[x1t[t]], w=[h2.t, ss.t])
                rstd_pow(rs, rs[:, 0:1], ss, ss[:, 0:1], 1, 1.0 / D)
                v_op(lambda e: e.tensor_scalar(out=h2[:], in0=xs, scalar1=rs[:, 0:1], scalar2=None, op0=ALU.mult), r=[x1t[t], rs.t], w=[h2.t])
                pt = nb()
                ptb = bfv(pt)
                for k in range(8):
                    tr(ptb[:, k * 128:(k + 1) * 128], h2[:, k * 128:(k + 1) * 128], identb[:], r=[h2.t, identb.t], w=[pt.t], sig=(k == 7))
                v_op(lambda e: e.tensor_tensor(out=hb[:, :, col0:col0 + 128], in0=ptb[:, :].rearrange("p (k n) -> p k n", k=8),
                                               in1=gfm[:, 16:24].unsqueeze(2).broadcast_to([128, 8, 128]), op=ALU.mult),
                     r=[pt.t, gfm.t], w=[hb.t])

            def ffn_out(i, t, a0, a1):
                xs = X1[:, t, :]
                ev = evs[i]
                lk = [T("lock0"), T("lock1")]
                a_op(lambda e: e.activation(out=h2[:, 0:512], in_=a0[:, :], func=AF.Square, accum_out=ss[:, 1:2]), r=[a0.t], w=[h2.t, ss.t, lk[0]])
                v_op(lambda e: e.tensor_copy(ev[:, 512:1024], a1[:, :]), r=[a1.t], w=[ev.t, lk[1]])
                a_op(lambda e: e.activation(out=h2[:, 0:512], in_=a1[:, :], func=AF.Square, accum_out=ss[:, 2:3]), r=[a1.t], w=[h2.t, ss.t, lk[1]])
                v_op(lambda e: e.tensor_copy(ev[:, 0:512], a0[:, :]), r=[a0.t], w=[ev.t, lk[0]])
                v_op(lambda e: e.tensor_tensor(out=ss[:, 1:2], in0=ss[:, 1:2], in1=ss[:, 2:3], op=ALU.add), r=[ss.t], w=[ss.t])
                rstd_pow(rs, rs[:, 1:2], ss, ss[:, 1:2], 1, 1.0 / D)
                v_op(lambda e: e.scalar_tensor_tensor(out=ev[:], in0=ev[:], scalar=rs[:, 1:2], in1=gpost[:], op0=ALU.mult, op1=ALU.mult),
                     r=[ev.t, rs.t, gpost.t], w=[ev.t])
                p_op(lambda e: e.tensor_tensor(out=xs, in0=xs, in1=ev[:], op=ALU.add), r=[x1t[t], ev.t], w=[x1t[t]])
                S.dma(QP, y_d[t * 128:(t + 1) * 128, :], xs, r=[x1t[t]])

            def load_cache():
                for c4 in range(11):
                    ci = cio[cio_n[0] % 2]
                    cio_n[0] += 1
                    S.dma(QS, ci[:], sconv_d[:, c4 * 512:(c4 + 1) * 512], w=[ci.t])
                    pt = nb()
                    for i in range(4):
                        tr(pt[:, i * 32:(i + 1) * 32], ci[:, i * 128:(i + 1) * 128], identf[0:32, 0:32],
                           r=[ci.t, identf.t], w=[pt.t], sig=(i == 3))
                    a_op(lambda e: e.copy(hist[:, c4 * 4:(c4 + 1) * 4, :].rearrange("p c r -> p (c r)"), pt[:, 0:128]), r=[pt.t], w=[hist.t])

            hn = [0]
            print("phaseB sbuf remaining", nc.sbuf_bytes_remaining)
            def cache_precompute():
                h4 = hist[:].rearrange("p c (j r) -> p c j r", r=2)
                w0b = convp[:, :, 0:1].unsqueeze(3).broadcast_to([128, 44, 16, 1])
                w1b = convp[:, :, 1:2].unsqueeze(3).broadcast_to([128, 44, 16, 1])
                p_op(lambda e: e.tensor_tensor(out=h4[:, :, :, 0:1], in0=h4[:, :, :, 0:1], in1=w0b, op=ALU.mult), r=[hist.t, convp.t], w=[hist.t])
                p_op(lambda e: e.tensor_tensor(out=histn[:].rearrange("p c (j r) -> p c j r", r=2)[:, :, :, 0:1], in0=h4[:, :, :, 1:2], in1=w1b, op=ALU.mult),
                     r=[hist.t, convp.t], w=[histn.t])
                p_op(lambda e: e.tensor_tensor(out=h4[:, :, :, 0:1], in0=h4[:, :, :, 0:1], in1=histn[:].rearrange("p c (j r) -> p c j r", r=2)[:, :, :, 0:1], op=ALU.add),
                     r=[hist.t, histn.t], w=[hist.t])
                p_op(lambda e: e.tensor_tensor(out=h4[:, :, :, 1:2], in0=h4[:, :, :, 1:2], in1=w0b, op=ALU.mult), r=[hist.t, convp.t], w=[hist.t])

            def up_stage(g, j, hb, tiles, N, sample):
                NW = N + 2
                blk = [b_ for b_, (j0, j1) in enumerate(jblk) if j0 <= j < j1][0]
                res = []
                for gv in range(2):
                    pu = nb()
                    c0 = gv * DFF + j * 128
                    for k in range(8):
                        mm(pu[:, 0:NW], w_up[:, k, c0:c0 + 128], hb[:, k, 0:NW], k == 0, k == 7,
                           r=[hb.t, w_up_t[gv][blk][k]], w=[pu.t], sig=(k == 7))
                    dst = (cg if gv == 0 else cv)[j % 2]
                    cc = gv * NJ + j
                    w0, w1, w2, bb = (convp[:, cc, r:r + 1] for r in range(4))
                    if not sample:
                        a_op(lambda e: e.activation(out=dst[:, 0:N], in_=pu[:, 2:2 + N], func=AF.Identity, scale=w2, bias=bb),
                             r=[pu.t, convp.t], w=[dst.t])
                        v_op(lambda e: e.scalar_tensor_tensor(out=dst[:, 0:N], in0=pu[:, 1:1 + N], scalar=w1, in1=dst[:, 0:N], op0=ALU.mult, op1=ALU.add),
                             r=[pu.t, convp.t, dst.t], w=[dst.t])
                        v_op(lambda e: e.scalar_tensor_tensor(out=dst[:, 0:N], in0=pu[:, 0:N], scalar=w0, in1=dst[:, 0:N], op0=ALU.mult, op1=ALU.add),
                             r=[pu.t, convp.t, dst.t], w=[dst.t])
                        if g == 7:
                            v_op(lambda e: e.tensor_copy(ulp2[:, cc, :], pu[:, N:N + 2]), r=[pu.t, dst.t], w=[ulp2.t])
                    else:
                        u3 = pu[:, 2:2 + N].rearrange("p (j i) -> p j i", i=8)
                        d3 = dst[:, 0:N].rearrange("p (j i) -> p j i", i=8)
                        h3 = hist[:, cc, :].rearrange("p (j r) -> p j r", r=2)
                        a_op(lambda e: e.activation(out=dst[:, 0:N], in_=pu[:, 2:2 + N], func=AF.Identity, scale=w2, bias=bb),
                             r=[pu.t, convp.t], w=[dst.t])
                        v_op(lambda e: e.scalar_tensor_tensor(out=d3[:, :, 1:8], in0=u3[:, :, 0:7], scalar=w1, in1=d3[:, :, 1:8], op0=ALU.mult, op1=ALU.add),
                             r=[pu.t, convp.t, dst.t], w=[dst.t])
                        v_op(lambda e: e.scalar_tensor_tensor(out=d3[:, :, 2:8], in0=u3[:, :, 0:6], scalar=w0, in1=d3[:, :, 2:8], op0=ALU.mult, op1=ALU.add),
                             r=[pu.t, convp.t, dst.t], w=[dst.t])
                        v_op(lambda e: e.tensor_tensor(out=d3[:, :, 0:2], in0=d3[:, :, 0:2], in1=h3[:, :, 0:2], op=ALU.add),
                             r=[hist.t, dst.t], w=[dst.t])
                        v_op(lambda e: e.tensor_copy(histn[:, cc, :].rearrange("p (j r) -> p j r", r=2), u3[:, :, 6:8]), r=[pu.t, dst.t], w=[histn.t])
                    res.append(dst)
                gb, vb = res
                a_op(lambda e: e.activation(out=gb[:, 0:N], in_=gb[:, 0:N], func=AF.Gelu_apprx_tanh), r=[gb.t], w=[gb.t])
                hj = hT[hn[0] % 3]
                hn[0] += 1
                p_op(lambda e: e.tensor_tensor(out=hj[:, 0:N], in0=gb[:, 0:N], in1=vb[:, 0:N], op=ALU.mult), r=[gb.t, vb.t], w=[hj.t])
                return hj

            def down_stage(j, hj, tiles):
                wd = load_wd(j)
                for i, t in enumerate(tiles):
                    for c in range(2):
                        a = acc[i * 2 + c]
                        mm(a[:, :], hj[:, i * 128:(i + 1) * 128], wd[:, c * 512:(c + 1) * 512], j == 0, j == NJ - 1,
                           r=[hj.t, wd.t], w=[a.t], sig=(j == NJ - 1))

            def group_tail(g, tiles, sample):
                for i, t in enumerate(tiles):
                    ffn_out(i, t, acc[i * 2], acc[i * 2 + 1])
                if g == 7 or sample:
                    nr = 32 if sample else 2
                    for c4 in range(11):
                        pt = nb()
                        for i in range(4):
                            c = c4 * 4 + i
                            src = histn[:, c, :] if sample else ulp2[:, c, :]
                            tr(pt[0:nr, i * 128:(i + 1) * 128], src, ident, r=[histn.t if sample else ulp2.t, identf.t], w=[pt.t], sig=(i == 3))
                        co = cio[cio_n[0] % 2]
                        cio_n[0] += 1
                        a_op(lambda e: e.copy(co[0:nr, :], pt[0:nr, :]), r=[pt.t], w=[co.t])
                        if sample:
                            S.dma(QS, cvs_d[:, c4 * 512:(c4 + 1) * 512], co[:, :], r=[co.t])
                        else:
                            S.dma(QS, cvp_d[:, c4 * 512:(c4 + 1) * 512], co[0:2, :], r=[co.t])

            def group_head(g):
                sample = (g == 8)
                tiles = [ST] if sample else [2 * g, 2 * g + 1]
                N = 128 * len(tiles)
                hb = hbs[g % 2]
                hprev = hbs[(g + 1) % 2]
                if sample or g == 0:
                    v_op(lambda e: e.memset(hb[:, :, 0:2], 0.0), w=[hb.t])
                else:
                    v_op(lambda e: e.tensor_copy(hb[:, :, 0:2], hprev[:, :, 256:258]), r=[hprev.t], w=[hb.t])
                for i, t in enumerate(tiles):
                    ffn_norm_T(t, hb, 2 + i * 128)
                return hb, tiles, N, sample

            pend = []

            def flush_one():
                pg, pj, phj, ptiles, psample = pend.pop(0)
                down_stage(pj, phj, ptiles)
                if pj == NJ - 1:
                    group_tail(pg, ptiles, psample)

            nxt = group_head(0)
            for g in range(9):
                hb, tiles, N, sample = nxt
                for j in range(NJ):
                    hj = up_stage(g, j, hb, tiles, N, sample)
                    pend.append((g, j, hj, tiles, sample))
                    if len(pend) > 2:
                        flush_one()
                    if j == 8 and g + 1 < 9:
                        nxt = group_head(g + 1)
                    if g == 1 and j == 14:
                        load_cache()
                        cache_precompute()
            while pend:
                flush_one()
            S.barrier(dma=True)
    return nc


def _consts():
    c = np.zeros((2, 128, NCONST), np.float32)
    idx = np.arange(128)
    for ty in range(2):
        c[ty, :, C_ID:C_ID + 128] = np.eye(128, dtype=np.float32)
        if ty == 0:
            same = np.ones((128, 128), bool)
            loc = idx
            clen = 128
            c[ty, 127, C_SEL:C_SEL + 128] = 1.0
        else:
            same = (idx[:, None] // 8) == (idx[None, :] // 8)
            loc = idx % 8
            clen = 8
        mask = (idx[None, :] >= idx[:, None]) & same
        c[ty, :, C_MT:C_MT + 128] = mask
        for h in range(4):
            lg = np.log(np.float64(GAM[h]))
            R = np.exp(-(loc[:, None] + 1.0) * lg) * mask
            c[ty, :, C_R + h * 128:C_R + (h + 1) * 128] = R
            c[ty, :, C_XI + h * 128:C_XI + (h + 1) * 128] = np.exp((loc[None, :] + 1.0) * lg)
            c[ty, :, C_Z + h] = np.exp((clen - 1.0 - loc) * lg)
        c[ty, :, C_RM:C_RM + 16] = (idx[:, None] // 8) == np.arange(16)[None, :]
        c[ty, :, C_FI:C_FI + 16] = idx[:, None] == 8 * np.arange(16)[None, :]
    return c


def _rot():
    freqs = 10000.0 ** (-np.arange(0, 128, 2, dtype=np.float64) / 128)
    r = np.zeros((NT, 128, 256), np.float32)
    for t in range(NT):
        if t == ST:
            pos = (16384 + (np.arange(128) % 8)).astype(np.float64)
        else:
            pos = (t * 128 + np.arange(128)).astype(np.float64)
        ang = pos[:, None] * freqs[None, :]
        cs, sn = np.cos(ang), np.sin(ang)
        r[t, :, 0:64] = cs
        r[t, :, 64:128] = sn
        r[t, :, 128:192] = cs * SCALE
        r[t, :, 192:256] = sn * SCALE
    return r


_CACHE = {}


def _prepare(x_prompt, x_sample, state_ret, state_mlstm_C, state_mlstm_n, state_mlstm_m,
             cache_ffn_conv, pre_mix_gain, w_in, b_gates, ret_head_gain, mlstm_head_gain,
             w_out, post_mix_gain, pre_ffn_gain, w_up, conv_w, conv_b, w_down, post_ffn_gain, cores=range(8)):
    f = lambda a: np.ascontiguousarray(np.asarray(a, dtype=np.float32))
    xp, xs = f(x_prompt), f(x_sample)
    hg = np.concatenate([f(ret_head_gain)[0], f(mlstm_head_gain)[0]])
    gfm = np.concatenate([f(pre_mix_gain)[0].reshape(8, 128).T, hg.reshape(8, 128).T,
                          f(pre_ffn_gain)[0].reshape(8, 128).T], axis=1)
    gpost = np.stack([f(post_mix_gain)[0], f(post_ffn_gain)[0]])
    cw, cb = f(conv_w)[0], f(conv_b)[0]
    convp = np.stack([cw[0], cw[1], cw[2], cb], axis=-1).reshape(44, 128, 4).transpose(1, 0, 2).reshape(128, 176)
    bm = np.tile(((np.arange(128)[None, :] // 8) == np.arange(16)[:, None]).astype(np.float32).reshape(1, 2048), (128, 1))
    shared = dict(w_in=f(w_in)[0], w_out=f(w_out)[0], w_up=f(w_up)[0], w_down=f(w_down)[0],
                  gfm=np.ascontiguousarray(gfm), gpost=gpost, bg=f(b_gates), convp=np.ascontiguousarray(convp),
                  consts=_consts(), bm=bm, rot=_rot())
    sr, sc, sn, sm, cc = f(state_ret)[0], f(state_mlstm_C)[0], f(state_mlstm_n)[0], f(state_mlstm_m)[0], f(cache_ffn_conv)[0]
    in_maps = []
    for c in cores:
        sl = slice(16 * c, 16 * c + 16)
        m = dict(shared)
        m["x"] = np.concatenate([xp[c], xs[sl].reshape(128, D)], axis=0)
        m["sret"], m["sC"], m["sn"], m["sm"] = sr[sl], sc[sl], sn[sl], sm[sl]
        m["sconv"] = np.ascontiguousarray(cc[sl].reshape(32, 2 * DFF))
        in_maps.append(m)
    return in_maps


def _assemble(res):
    g = lambda n: [np.asarray(r[n]) for r in res]
    y = g("y")
    outs = (
        np.stack([a[:2048] for a in y]),
        np.concatenate([a[2048:].reshape(16, 8, D) for a in y]),
        np.stack(g("Sp"))[None], np.concatenate(g("Ss"))[None],
        np.stack(g("Cp"))[None], np.concatenate(g("Cs"))[None],
        np.stack(g("np"))[None], np.concatenate(g("ns"))[None],
        np.stack([a[0] for a in g("mp")])[None], np.concatenate(g("ms"))[None],
        np.stack(g("cvp"))[None], np.concatenate([a.reshape(16, 2, 2 * DFF) for a in g("cvs")])[None],
    )
    return tuple(np.ascontiguousarray(o, dtype=np.float32) for o in outs)


def kernel(**inputs):
    if "nc" not in _CACHE:
        _CACHE["nc"] = build()
    in_maps = _prepare(**inputs)
    res = run_bass_kernel_spmd(_CACHE["nc"], in_maps, core_ids=list(range(8))).results
    return _assemble(res)
```

```python
import contextlib
import numpy as np
import concourse.bass as bass
import concourse.mybir as mybir
from concourse.bass_utils import run_bass_kernel_spmd
from concourse.alu_op_type import AluOpType as ALU

F32 = mybir.dt.float32
BF16 = mybir.dt.bfloat16
AF = mybir.ActivationFunctionType
AXX = mybir.AxisListType.X

D = 1024
NT = 17
ST = 16
INC = 4104
DFF = 2816
NJ = 22
EPS = 1e-6
SCALE = 128 ** -0.5
GAM = [1.0 - 2.0 ** (-5.0 - h) for h in range(4)]

C_ID = 0
C_MT = 128
C_SEL = 256
C_R = 384
C_XI = 896
C_Z = 1408
C_RM = 1412
C_FI = 1428
NCONST = 1444

DEBUG = {}


class T:
    __slots__ = ("name", "w", "r")

    def __init__(self, name=""):
        self.name = name
        self.w = None
        self.r = {}


class Eng:
    def __init__(self, name, obj, sem, sig_all):
        self.name, self.obj, self.sem, self.sig_all = name, obj, sem, sig_all
        self.count = 0
        self.seen = {}
        self.pending = False


class Sched:
    def __init__(self, nc, es):
        self.nc = nc
        mk = lambda n: es.enter_context(nc.semaphore(n))
        self.pe = Eng("pe", nc.tensor, mk("s_pe"), False)
        self.act = Eng("act", nc.scalar, mk("s_act"), True)
        self.dve = Eng("dve", nc.vector, mk("s_dve"), True)
        self.pool = Eng("pool", nc.gpsimd, mk("s_pool"), True)
        self.sp = Eng("sp", nc.sync, None, False)
        self.engs = [self.pe, self.act, self.dve, self.pool, self.sp]
        self.q_sp = dict(eng=self.sp, sems=[mk("q_sp%d" % i) for i in range(16)], issued=[0] * 16, n=0, id=0)
        self.q_pool = dict(eng=self.pool, sems=[mk("q_po%d" % i) for i in range(12)], issued=[0] * 12, n=0, id=1)
        self.qs = [self.q_sp, self.q_pool]

    label = ""
    pe_log = []
    pe_waits = []
    prod = {}

    def _wait(self, e, sem, val):
        if e.seen.get(id(sem), 0) < val:
            e.obj.wait_ge(sem, val)
            e.seen[id(sem)] = val
            if e is self.pe:
                Sched.pe_waits.append((len(Sched.pe_log), Sched.prod.get((id(sem), val), "dma/?")))

    def _deps(self, e, r, w):
        waits = {}

        def need(rec):
            if rec is None:
                return
            sem, val = rec
            if e is self.pe and sem is self.pe.sem:
                return
            if waits.get(id(sem), (None, 0))[1] < val:
                waits[id(sem)] = (sem, val)
        for b in r:
            need(b.w)
        for b in w:
            need(b.w)
            for rec in b.r.values():
                need(rec)
        for sem, val in waits.values():
            self._wait(e, sem, val)

    def op(self, e, fn, r=(), w=(), sig=None):
        self._deps(e, r, w)
        inst = fn(e.obj)
        signal = e.sig_all if sig is None else sig
        if e is self.pe:
            Sched.pe_log.append(Sched.label)
        if signal:
            e.count += 1
            inst.then_inc(e.sem, 1)
            rec = (e.sem, e.count)
            e.pending = False
            Sched.prod[(id(e.sem), e.count)] = e.name + ":" + Sched.label
        else:
            rec = (e.sem, e.count + 1)
            e.pending = True
        for b in w:
            b.w = rec
            b.r = {}
        for b in r:
            b.r[e.name] = rec
        return inst

    def dma(self, q, out, in_, r=(), w=(), **kw):
        e = q["eng"]
        slot = q["n"] % len(q["sems"])
        q["n"] += 1
        sem = q["sems"][slot]
        if q["issued"][slot] > 0:
            self._wait(e, sem, 16 * q["issued"][slot])
        self._deps(e, r, w)
        e.obj.dma_start(out=out, in_=in_, **kw).then_inc(sem, 16)
        q["issued"][slot] += 1
        rec = (sem, 16 * q["issued"][slot])
        for b in w:
            b.w = rec
            b.r = {}
        for b in r:
            b.r["q%d_%d" % (q["id"], slot)] = rec

    def barrier(self, dma=True):
        assert not self.pe.pending
        for e in self.engs:
            for f in self.engs:
                if f is not e and f.sem is not None and f.count > 0:
                    self._wait(e, f.sem, f.count)
            if dma:
                for q in self.qs:
                    for sem, n in zip(q["sems"], q["issued"]):
                        if n > 0:
                            self._wait(e, sem, 16 * n)


class B:
    def __init__(self, ap, name=""):
        self.ap = ap
        self.t = T(name)

    def __getitem__(self, k):
        return self.ap[k]


def bc_rows(dram_ap_2d, row, ncols, nparts=128):
    return bass.AP(dram_ap_2d.tensor, row * ncols, [[0, nparts], [1, ncols]])


def build(dbg=()):
    nc = bass.Bass("TRN2", target_bir_lowering=False)
    din = lambda n, s, dt=F32: nc.dram_tensor(n, s, dt, kind="ExternalInput").ap()
    dout = lambda n, s: nc.dram_tensor(n, s, F32, kind="ExternalOutput").ap()
    x_d = din("x", [NT * 128, D])
    sret_d = din("sret", [16, 4, 128, 128])
    sC_d = din("sC", [16, 4, 128, 128])
    sn_d = din("sn", [16, 4, 128])
    sm_d = din("sm", [16, 4])
    sconv_d = din("sconv", [32, 2 * DFF])
    w_in_d = din("w_in", [D, INC])
    w_out_d = din("w_out", [D, D])
    w_up_d = din("w_up", [D, 2 * DFF])
    w_down_d = din("w_down", [DFF, D])
    gfm_d = din("gfm", [128, 24])
    gpost_d = din("gpost", [2, D])
    bg_d = din("bg", [1, 8])
    convp_d = din("convp", [128, 44 * 4])
    cst_d = din("consts", [2, 128, NCONST])
    bm_d = din("bm", [128, 2048])
    rot_d = din("rot", [NT, 128, 256])

    y_d = dout("y", [NT * 128, D])
    Sp_d = dout("Sp", [4, 128, 128])
    Ss_d = dout("Ss", [16, 4, 128, 128])
    Cp_d = dout("Cp", [4, 128, 128])
    Cs_d = dout("Cs", [16, 4, 128, 128])
    np_d = dout("np", [4, 128])
    ns_d = dout("ns", [16, 4, 128])
    mp_d = dout("mp", [1, 4])
    ms_d = dout("ms", [16, 4])
    cvp_d = dout("cvp", [2, 2 * DFF])
    cvs_d = dout("cvs", [32, 2 * DFF])
    dbg_d = {n: dout("dbg_" + n, s) for n, s in dbg}

    wout_scr = nc.dram_tensor("wout_scr", [D, D], BF16, kind="Internal").ap()
    wdn_scr = nc.dram_tensor("wdn_scr", [DFF, D], BF16, kind="Internal").ap()
    wup_scr = nc.dram_tensor("wup_scr", [D, 2 * DFF], BF16, kind="Internal").ap()

    with contextlib.ExitStack() as es:
        S = Sched(nc, es)
        PE, ACT, DVE, POOL = S.pe, S.act, S.dve, S.pool
        QS, QP = S.q_sp, S.q_pool

        def sb(stack, name, shape, dt=F32):
            return B(stack.enter_context(nc.sbuf_tensor("sb_" + name, shape, dt)), name)

        def v_op(fn, r=(), w=()):
            return S.op(DVE, fn, r, w)

        def a_op(fn, r=(), w=()):
            return S.op(ACT, fn, r, w)

        def p_op(fn, r=(), w=()):
            return S.op(POOL, fn, r, w)

        def mm(out, lhsT, rhs, start, stop, r=(), w=(), sig=False):
            return S.op(PE, lambda e: e.matmul(out, lhsT=lhsT, rhs=rhs, start=start, stop=stop), r, w, sig)

        def tr(out, in_, ident, r=(), w=(), sig=False):
            return S.op(PE, lambda e: e.transpose(out, in_, ident), r, w, sig)

        def dump(name, ap, tt):
            if name in dbg_d:
                S.dma(QS, dbg_d[name], ap, r=tt)

        banks = [B(es.enter_context(nc.psum_tensor("bank%d" % i, [128, 512], F32)), "bank%d" % i) for i in range(8)]
        ring = {"i": 0, "n": 8, "base": 0, "held": set()}

        def nb(hold=False):
            while True:
                idx = ring["base"] + ring["i"] % ring["n"]
                ring["i"] += 1
                if idx not in ring["held"]:
                    break
            if hold:
                ring["held"].add(idx)
            return banks[idx]

        def rel(bank):
            ring["held"].discard(banks.index(bank))

        def bfv(bank):
            return bank.ap.bitcast(BF16)

        X1 = sb(es, "X1", [128, NT, D])
        x1t = [T("x1_%d" % t) for t in range(NT)]
        identf = sb(es, "identf", [128, 128])
        identb = sb(es, "identb", [128, 128], BF16)
        gfm = sb(es, "gfm", [128, 24])
        gpost = sb(es, "gpost", [128, D])
        nhalf = sb(es, "nhalf", [128, 8])
        eps_t = sb(es, "eps_t", [128, 1])
        ss = sb(es, "ss", [128, 4])
        rs = sb(es, "rs", [128, 4])
        scr_t = [T("wout_scr")]
        wdn_t = [T("wdn_scr%d" % i) for i in range(NJ)]
        wups_t = [[T("wup_scr%d_%d" % (k, hc)) for hc in range(4)] for k in range(8)]
        wup_casts = []

        def issue_wup_casts(n):
            for _ in range(n):
                if wup_casts:
                    kind, k, hc = wup_casts.pop(0)
                    if kind == "d":
                        S.dma(QP, wdn_scr[k * 128:(k + 1) * 128, :], w_down_d[k * 128:(k + 1) * 128, :], w=[wdn_t[k]])
                    else:
                        c0, c1 = hc * 1408, (hc + 1) * 1408
                        S.dma(QP, wup_scr[k * 128:(k + 1) * 128, c0:c1], w_up_d[k * 128:(k + 1) * 128, c0:c1], w=[wups_t[k][hc]])
        jblk = [(0, 6), (6, 12), (12, 17), (17, 22)]
        ident = identf[:, :]

        def rstd_pow(out_b, out_ap, in_b, in_ap, ncol, scale):
            v_op(lambda e: e.tensor_scalar(out=out_ap, in0=in_ap, scalar1=scale, scalar2=EPS, op0=ALU.mult, op1=ALU.add),
                 r=[in_b.t], w=[out_b.t])
            p_op(lambda e: e.tensor_tensor(out=out_ap, in0=out_ap, in1=nhalf[:, 0:ncol], op=ALU.pow),
                 r=[out_b.t, nhalf.t], w=[out_b.t])

        S.dma(QS, identf[:], cst_d[0, :, C_ID:C_ID + 128], w=[identf.t])
        S.dma(QS, gfm[:], gfm_d[:, :], w=[gfm.t])
        S.dma(QS, gpost[:], bc_rows(gpost_d, 0, D), w=[gpost.t])
        v_op(lambda e: e.memset(nhalf[:], -0.5), w=[nhalf.t])
        v_op(lambda e: e.memset(eps_t[:], EPS), w=[eps_t.t])
        v_op(lambda e: e.tensor_copy(identb[:], ident), r=[identf.t], w=[identb.t])

        with contextlib.ExitStack() as sa:
            cst = sb(sa, "cst", [128, NCONST])
            S.dma(QS, cst[:], cst_d[1], w=[cst.t])
            w_in = sb(sa, "w_in_bf", [128, 8, INC], BF16)
            w_in_t = [[T("w_in_%d_%d" % (k, c)) for c in range(3)] for k in range(8)]
            cblk = [(0, 1536), (1536, 3072), (3072, INC)]
            stg = [X1.ap[:, 2 * i:2 * i + 2, :].rearrange("p t c -> p (t c)") for i in range(8)]
            stg_t = [T("stg%d" % i) for i in range(8)]
            npc = 0
            for c in (2, 0, 1):
                c0, c1 = cblk[c]
                for k in range(8):
                    if k % 2 == 0:
                        S.dma(QP, w_in[:, k, c0:c1], w_in_d[k * 128:(k + 1) * 128, c0:c1], w=[w_in_t[k][c]])
                    else:
                        i = npc % 8
                        S.dma(QS, stg[i][:, 0:c1 - c0], w_in_d[k * 128:(k + 1) * 128, c0:c1], w=[stg_t[i]])
                        if npc % 2 == 0:
                            v_op(lambda e: e.tensor_copy(w_in[:, k, c0:c1], stg[i][:, 0:c1 - c0]), r=[stg_t[i]], w=[w_in_t[k][c]])
                        else:
                            a_op(lambda e: e.copy(w_in[:, k, c0:c1], stg[i][:, 0:c1 - c0]), r=[stg_t[i]], w=[w_in_t[k][c]])
                        npc += 1
            S.dma(QP, wout_scr[:, :], w_out_d[:, :], w=[scr_t[0]])
            wo_ring = [sb(sa, "wo%d" % i, [128, D], BF16) for i in range(3)]
            wo_n = [0]
            wo_q = []
            bgb = sb(sa, "bgb", [128, 8])
            S.dma(QS, bgb[:], bc_rows(bg_d, 0, 8), w=[bgb.t])
            rot = [sb(sa, "rot%d" % i, [128, 256]) for i in range(2)]
            h_bf = sb(sa, "h_bf", [128, D], BF16)
            hm_bf = sb(sa, "mix_bf", [128, D], BF16)
            hTs = [sb(sa, "hTf%d" % i, [128, 8, 128], BF16) for i in range(2)]
            mixT = sb(sa, "mixT", [128, 8, 128], BF16)
            qkf = sb(sa, "qkf", [128, 3, 512], BF16)
            v_m = sb(sa, "v_m", [128, 4, 128], BF16)
            gtmp = sb(sa, "gtmp", [128, 512])
            sT = [sb(sa, "sT%d" % i, [128, 4, 128], BF16) for i in range(2)]
            hout = sb(sa, "hout", [128, 512])
            gtmpb = hout
            rtmp = B(gtmp.ap[:, 0:256].rearrange("p (h d) -> p h d", h=4), "rtmp")
            rtmp2 = B(gtmp.ap[:, 256:512].rearrange("p (h d) -> p h d", h=4), "rtmp2")
            rtmp.t = gtmp.t
            rtmp2.t = gtmp.t
            g8 = sb(sa, "g8", [128, 8])
            lt = sb(sa, "lt", [128, 4])
            Bl = [sb(sa, "Bl%d" % i, [128, 4]) for i in range(2)]
            Av = sb(sa, "Av", [128, 4])
            AT = sb(sa, "AT", [4, 128])
            ulast = [sb(sa, "ulast%d" % i, [4, 16]) for i in range(2)]
            ultx = sb(sa, "ultx", [4, 128])
            UL = [sb(sa, "UL%d" % i, [128, 4]) for i in range(2)]
            d12 = sb(sa, "d12", [128, 12])
            st6 = sb(sa, "st6", [128, 8, 6])
            mv = sb(sa, "mv", [128, 8, 2])
            gsc = sb(sa, "gsc", [128, 8])
            gbi = sb(sa, "gbi", [128, 8])
            den = sb(sa, "den", [128, 4])
            rden = sb(sa, "rden", [128, 4])
            n16buf = sb(sa, "n16buf", [128, 512])
            n16 = B(n16buf.ap[0:16, :].rearrange("p (h d) -> p h d", h=4), "n16")
            n16.t = n16buf.t
            wo3 = B(n16buf.ap.bitcast(BF16), "wo3")
            wo3.t = n16buf.t
            ssb = sb(sa, "ssb", [128, 4])
            rsb = sb(sa, "rsb", [128, 4])

            def mkset(stack, i):
                d = {}
                d["km"] = sb(stack, "km%d" % i, [128, 512], BF16)
                d["kz"] = sb(stack, "kz%d" % i, [128, 4, 128], BF16)
                d["v_r"] = sb(stack, "v_r%d" % i, [128, 4, 128], BF16)
                d["vE"] = sb(stack, "vE%d" % i, [128, 4, 128], BF16)
                d["gate"] = sb(stack, "gate%d" % i, [128, D])
                d["qkT"] = sb(stack, "qkT%d" % i, [128, 16, 128], BF16)
                d["e12"] = sb(stack, "e12_%d" % i, [128, 12])
                d["ea_bf"] = sb(stack, "ea_bf%d" % i, [128, 4], BF16)
                d["mrow"] = sb(stack, "mrow%d" % i, [128, 4])
                return d

            sets = [mkset(sa, 0)]

            def wo_prefetch(n):
                for _ in range(n):
                    if not wo_ring:
                        return
                    c = wo_n[0]
                    b = wo_ring.pop(0)
                    wo_n[0] += 1
                    k = c % 8
                    S.dma(QS, b[:, :], wout_scr[k * 128:(k + 1) * 128, :], r=[scr_t[0]], w=[b.t])
                    wo_q.append(b)

            def load_wo(k):
                if not wo_q:
                    wo_prefetch(1)
                b = wo_q.pop(0)
                return b

            def nstage(t):
                xs = X1[:, t, :]
                hmT = hTs[t % 2]
                a_op(lambda e: e.activation(out=h_bf[:], in_=xs, func=AF.Square, accum_out=ss[:, 0:1]),
                     r=[x1t[t]], w=[h_bf.t, ss.t])
                rstd_pow(rs, rs[:, 0:1], ss, ss[:, 0:1], 1, 1.0 / D)
                a_op(lambda e: e.activation(out=h_bf[:], in_=xs, func=AF.Identity, scale=rs[:, 0:1]),
                     r=[x1t[t], rs.t], w=[h_bf.t])

            def nstage2(t):
                hmT = hTs[t % 2]
                pt = nb()
                ptb = bfv(pt)
                for k in range(8):
                    tr(ptb[:, k * 128:(k + 1) * 128], h_bf[:, k * 128:(k + 1) * 128], identb[:],
                       r=[h_bf.t, identb.t], w=[pt.t], sig=(k == 7))
                v_op(lambda e: e.tensor_tensor(out=hmT[:], in0=ptb[:, :].rearrange("p (k n) -> p k n", k=8),
                                               in1=gfm[:, 0:8].unsqueeze(2).broadcast_to([128, 8, 128]), op=ALU.mult),
                     r=[pt.t, gfm.t], w=[hmT.t])

            def front(t, cur, ctx):
                sample = (t == ST)
                rt = rot[t % 2]
                hmT = hTs[t % 2]
                km, kz, v_r, vE, gate, qkT, e12, ea_bf, mrow = (cur[n] for n in ("km", "kz", "v_r", "vE", "gate", "qkT", "e12", "ea_bf", "mrow"))
                S.dma(QS, rt[:], rot_d[t], w=[rt.t])

                def proj(g, ncols=512):
                    pb = nb()
                    c0 = g * 512
                    for k in range(8):
                        mm(pb[:, 0:ncols], hmT[:, k, :], w_in[:, k, c0:c0 + ncols], k == 0, k == 7,
                           r=[hmT.t, w_in_t[k][g // 3]], w=[pb.t], sig=(k == 7))
                    return pb

                def rotary(pb, dst, ci):
                    pv = pb[:, :].rearrange("p (h d) -> p h d", h=4)
                    x1_, x2_ = pv[:, :, 0:64], pv[:, :, 64:128]
                    cc = rt[:, ci * 64:(ci + 1) * 64].unsqueeze(1).broadcast_to([128, 4, 64])
                    sn_ = rt[:, (ci + 1) * 64:(ci + 2) * 64].unsqueeze(1).broadcast_to([128, 4, 64])
                    dv = dst.rearrange("p (h d) -> p h d", h=4)
                    v_op(lambda e: e.tensor_tensor(out=rtmp[:], in0=x1_, in1=cc, op=ALU.mult), r=[pb.t, rt.t], w=[rtmp.t])
                    v_op(lambda e: e.tensor_tensor(out=rtmp2[:], in0=x2_, in1=sn_, op=ALU.mult), r=[pb.t, rt.t], w=[rtmp2.t])
                    p_op(lambda e: e.tensor_tensor(out=dv[:, :, 0:64], in0=rtmp[:], in1=rtmp2[:], op=ALU.subtract),
                         r=[rtmp.t, rtmp2.t], w=[qkf.t])
                    v_op(lambda e: e.tensor_tensor(out=rtmp[:], in0=x1_, in1=sn_, op=ALU.mult), r=[pb.t, rt.t], w=[rtmp.t])
                    v_op(lambda e: e.tensor_tensor(out=rtmp2[:], in0=x2_, in1=cc, op=ALU.mult), r=[pb.t, rt.t], w=[rtmp2.t])
                    p_op(lambda e: e.tensor_tensor(out=dv[:, :, 64:128], in0=rtmp[:], in1=rtmp2[:], op=ALU.add),
                         r=[rtmp.t, rtmp2.t], w=[qkf.t])

                def sigmoid_from(pb):
                    a_op(lambda e: e.activation(out=gtmp[:], in_=pb[:, :], func=AF.Exp, scale=-1.0), r=[pb.t], w=[gtmp.t])
                    a_op(lambda e: e.activation(out=gtmp[:], in_=gtmp[:], func=AF.Ln, bias=1.0), r=[gtmp.t], w=[gtmp.t])
                    a_op(lambda e: e.activation(out=gtmp[:], in_=gtmp[:], func=AF.Exp, scale=-1.0), r=[gtmp.t], w=[gtmp.t])

                Bc, Bp = Bl[t % 2], Bl[(t + 1) % 2]
                ULc, ULp = UL[t % 2], UL[(t + 1) % 2]
                ulc, ulp = ulast[t % 2], ulast[(t + 1) % 2]
                first = sample or t == 0
                pb = proj(8, 8)
                v_op(lambda e: e.tensor_tensor(out=g8[:], in0=pb[:, 0:8], in1=bgb[:], op=ALU.add), r=[pb.t, bgb.t], w=[g8.t])
                a_op(lambda e: e.activation(out=lt[:], in_=g8[:, 4:8], func=AF.Exp, scale=-1.0), r=[g8.t], w=[lt.t])
                a_op(lambda e: e.activation(out=lt[:], in_=lt[:], func=AF.Ln, bias=1.0), r=[lt.t], w=[lt.t])
                yield
                pb = proj(0)
                rotary(pb, qkf[:, 0, :], 0)
                yield
                pb = proj(1)
                rotary(pb, qkf[:, 1, :], 2)
                p_op(lambda e: e.tensor_tensor(out=kz[:], in0=qkf[:, 1, :].rearrange("p (h d) -> p h d", h=4),
                                               in1=cst[:, C_Z:C_Z + 4].unsqueeze(2).broadcast_to([128, 4, 128]), op=ALU.mult),
                     r=[qkf.t, cst.t], w=[kz.t])
                yield
                pb = proj(2)
                a_op(lambda e: e.copy(v_r[:].rearrange("p h d -> p (h d)"), pb[:, :]), r=[pb.t], w=[v_r.t])
                pm = nb()
                mm(pm[:, 0:4], cst[:, C_MT:C_MT + 128], lt[:], True, first, r=[cst.t, lt.t], w=[pm.t], sig=first)
                if not first:
                    mm(pm[:, 0:4], cst[:, C_SEL:C_SEL + 128], Bp[:], False, True, r=[cst.t, Bp.t], w=[pm.t], sig=True)
                v_op(lambda e: e.tensor_copy(Bc[:], pm[:, 0:4]), r=[pm.t], w=[Bc.t])
                v_op(lambda e: e.tensor_tensor(out=Av[:], in0=g8[:, 0:4], in1=Bc[:], op=ALU.add), r=[g8.t, Bc.t], w=[Av.t])
                yield
                pb = proj(3)
                sigmoid_from(pb)
                v_op(lambda e: e.tensor_tensor(out=gate[:, 0:512], in0=pb[:, :], in1=gtmp[:], op=ALU.mult),
                     r=[pb.t, gtmp.t], w=[gate.t])
                yield
                pt = nb()
                ptb = bfv(pt)
                for i in range(8):
                    tr(ptb[:, i * 128:(i + 1) * 128], qkf[:, i // 4, (i % 4) * 128:(i % 4 + 1) * 128], identb[:],
                       r=[qkf.t, identb.t], w=[pt.t], sig=(i == 7))
                v_op(lambda e: e.tensor_tensor(out=qkT[:, 0:4, :], in0=ptb[:, 0:512].rearrange("p (h n) -> p h n", h=4),
                                               in1=cst[:, C_XI:C_XI + 512].rearrange("p (h n) -> p h n", h=4), op=ALU.mult),
                     r=[pt.t, cst.t], w=[qkT.t])
                a_op(lambda e: e.copy(qkT[:, 4:8, :], ptb[:, 512:1024].rearrange("p (h n) -> p h n", h=4)),
                     r=[pt.t], w=[qkT.t])
                pm = nb()
                tr(pm[0:4, 0:128], Av[:], ident, r=[Av.t, identf.t], w=[pm.t], sig=True)
                v_op(lambda e: e.tensor_copy(AT[:], pm[0:4, 0:128]), r=[pm.t], w=[AT.t])
                if sample:
                    v_op(lambda e: e.tensor_reduce(out=ulc[:, :], in_=AT[:, :].rearrange("p (j i) -> p j i", i=8),
                                                   axis=AXX, op=ALU.max), r=[AT.t], w=[ulc.t])
                    v_op(lambda e: e.tensor_tensor(out=ulc[:, :], in0=ulc[:, :], in1=ctx["m0T"][:, :], op=ALU.max),
                         r=[ulc.t, ctx["m0T"].t], w=[ulc.t])
                    v_op(lambda e: e.tensor_copy(ultx[:, :].rearrange("p (j i) -> p j i", i=8),
                                                 ulc[:, :].unsqueeze(2).broadcast_to([4, 16, 8])), r=[ulc.t], w=[ultx.t])
                else:
                    v_op(lambda e: e.tensor_reduce(out=ulc[:, 0:1], in_=AT[:, :], axis=AXX, op=ALU.max),
                         r=[AT.t], w=[ulc.t])
                    if t > 0:
                        v_op(lambda e: e.tensor_tensor(out=ulc[:, 0:1], in0=ulc[:, 0:1], in1=ulp[:, 0:1], op=ALU.max),
                             r=[ulc.t, ulp.t], w=[ulc.t])
                    v_op(lambda e: e.tensor_copy(ultx[:, :], ulc[:, 0:1].broadcast_to([4, 128])), r=[ulc.t], w=[ultx.t])
                yield
                pb = proj(4)
                a_op(lambda e: e.copy(qkf[:, 2, :], pb[:, :]), r=[pb.t], w=[qkf.t])
                yield
                pb = proj(5)
                a_op(lambda e: e.mul(km[:], pb[:, :], SCALE), r=[pb.t], w=[km.t])
                pm = nb()
                tr(pm[:, 0:4], ultx[:, :], identf[0:4, 0:4], r=[ultx.t, identf.t], w=[pm.t], sig=True)
                v_op(lambda e: e.tensor_copy(ULc[:], pm[:, 0:4]), r=[pm.t], w=[ULc.t])
                ulprev = ctx["m0x"] if sample else ULp
                v_op(lambda e: e.tensor_tensor(out=d12[:, 0:4], in0=Av[:], in1=ULc[:], op=ALU.subtract), r=[Av.t, ULc.t], w=[d12.t])
                v_op(lambda e: e.tensor_tensor(out=d12[:, 4:8], in0=Bc[:], in1=ULc[:], op=ALU.subtract), r=[Bc.t, ULc.t], w=[d12.t])
                if sample or t > 0:
                    v_op(lambda e: e.tensor_tensor(out=d12[:, 8:12], in0=ulprev[:], in1=ULc[:], op=ALU.subtract),
                         r=[ulprev.t, ULc.t], w=[d12.t])
                else:
                    v_op(lambda e: e.memset(d12[:, 8:12], -100.0), w=[d12.t])
                a_op(lambda e: e.activation(out=e12[:], in_=d12[:], func=AF.Exp), r=[d12.t], w=[e12.t])
                v_op(lambda e: e.tensor_copy(ea_bf[:], e12[:, 0:4]), r=[e12.t], w=[ea_bf.t])
                v_op(lambda e: e.tensor_tensor(out=mrow[:], in0=ULc[:], in1=Bc[:], op=ALU.subtract), r=[ULc.t, Bc.t], w=[mrow.t])
                yield
                pb = proj(6)
                a_op(lambda e: e.copy(v_m[:].rearrange("p h d -> p (h d)"), pb[:, :]), r=[pb.t], w=[v_m.t])
                p_op(lambda e: e.tensor_tensor(out=vE[:], in0=v_m[:], in1=e12[:, 0:4].unsqueeze(2).broadcast_to([128, 4, 128]), op=ALU.mult),
                     r=[v_m.t, e12.t], w=[vE.t])
                yield
                pb = proj(7)
                sigmoid_from(pb)
                a_op(lambda e: e.copy(gate[:, 512:1024], gtmp[:]), r=[gtmp.t], w=[gate.t])
                yield
                pt = nb()
                ptb = bfv(pt)
                for i in range(8):
                    src = qkf[:, 2, i * 128:(i + 1) * 128] if i < 4 else km[:, (i - 4) * 128:(i - 3) * 128]
                    tr(ptb[:, i * 128:(i + 1) * 128], src, identb[:], r=[qkf.t, km.t, identb.t], w=[pt.t], sig=(i == 7))
                a_op(lambda e: e.copy(qkT[:, 8:16, :], ptb[:, :].rearrange("p (h n) -> p h n", h=8)),
                     r=[pt.t], w=[qkT.t])
                yield

            def back(t, cur, heads_gen):
                xs = X1[:, t, :]
                gate, qkT, e12 = cur["gate"], cur["qkT"], cur["e12"]
                for grp in range(2):
                    ps_ = nb()
                    qo, ko = (0, 4) if grp == 0 else (8, 12)
                    for h in range(4):
                        mm(ps_[:, h * 128:(h + 1) * 128], qkT[:, ko + h, :], qkT[:, qo + h, :], True, True,
                           r=[qkT.t], w=[ps_.t], sig=(h == 3))
                    if grp == 0:
                        v_op(lambda e: e.tensor_tensor(out=sT[0][:].rearrange("p h n -> p (h n)"), in0=ps_[:, :],
                                                       in1=cst[:, C_R:C_R + 512], op=ALU.mult), r=[ps_.t, cst.t], w=[sT[0].t])
                    else:
                        v_op(lambda e: e.tensor_tensor(out=sT[1][:], in0=ps_[:, :].rearrange("p (h n) -> p h n", h=4),
                                                       in1=cst[:, C_MT:C_MT + 128].unsqueeze(1).broadcast_to([128, 4, 128]), op=ALU.mult),
                             r=[ps_.t, cst.t], w=[sT[1].t])
                    yield
                res = {}
                for _ in heads_gen(t, cur, res):
                    yield
                po = [res["po0"], res["po1"]]
                pdd = res["pdd"]
                for h in range(4):
                    v_op(lambda e: e.bn_stats(st6[:, h, :], po[0][:, h * 128:(h + 1) * 128]), r=[po[0].t], w=[st6.t])
                    v_op(lambda e: e.bn_aggr(mv[:, h, :], st6[:, h, :]), r=[st6.t], w=[mv.t])
                v_op(lambda e: e.tensor_scalar(out=den[:], in0=pdd[:, 0:4], scalar1=-1.0, scalar2=None, op0=ALU.mult), r=[pdd.t], w=[den.t])
                v_op(lambda e: e.tensor_tensor(out=den[:], in0=den[:], in1=pdd[:, 0:4], op=ALU.max), r=[den.t, pdd.t], w=[den.t])
                v_op(lambda e: e.tensor_tensor(out=den[:], in0=den[:], in1=e12[:, 4:8], op=ALU.max), r=[den.t, e12.t], w=[den.t])
                v_op(lambda e: e.reciprocal(rden[:], den[:]), r=[den.t], w=[rden.t])
                if pdd in banks:
                    rel(pdd)
                yield
                for h in range(4):
                    v_op(lambda e: e.bn_stats(st6[:, 4 + h, :], po[1][:, h * 128:(h + 1) * 128]), r=[po[1].t], w=[st6.t])
                    v_op(lambda e: e.bn_aggr(mv[:, 4 + h, :], st6[:, 4 + h, :]), r=[st6.t], w=[mv.t])
                v_op(lambda e: e.tensor_tensor(out=mv[:, 4:8, 0], in0=mv[:, 4:8, 0], in1=rden[:], op=ALU.mult), r=[mv.t, rden.t], w=[mv.t])
                v_op(lambda e: e.tensor_tensor(out=mv[:, 4:8, 1], in0=mv[:, 4:8, 1], in1=rden[:], op=ALU.mult), r=[mv.t, rden.t], w=[mv.t])
                v_op(lambda e: e.tensor_tensor(out=mv[:, 4:8, 1], in0=mv[:, 4:8, 1], in1=rden[:], op=ALU.mult), r=[mv.t, rden.t], w=[mv.t])
                yield
                a_op(lambda e: e.activation(out=gsc[:], in_=mv[:, :, 1], func=AF.Ln, bias=eps_t[:, 0:1]), r=[mv.t, eps_t.t], w=[gsc.t])
                a_op(lambda e: e.activation(out=gsc[:], in_=gsc[:], func=AF.Exp, scale=-0.5), r=[gsc.t], w=[gsc.t])
                v_op(lambda e: e.scalar_tensor_tensor(out=gbi[:], in0=mv[:, :, 0], scalar=-1.0, in1=gsc[:], op0=ALU.mult, op1=ALU.mult),
                     r=[mv.t, gsc.t], w=[gbi.t])
                v_op(lambda e: e.tensor_tensor(out=gsc[:, 4:8], in0=gsc[:, 4:8], in1=rden[:], op=ALU.mult), r=[gsc.t, rden.t], w=[gsc.t])
                for grp in range(2):
                    for h in range(4):
                        a = grp * 4 + h
                        if grp == 0:
                            a_op(lambda e: e.activation(out=hout[:, h * 128:(h + 1) * 128], in_=po[grp][:, h * 128:(h + 1) * 128],
                                                        func=AF.Identity, scale=gsc[:, a:a + 1], bias=gbi[:, a:a + 1]),
                                 r=[po[grp].t, gsc.t, gbi.t], w=[hout.t])
                        else:
                            v_op(lambda e: e.tensor_scalar(out=hout[:, h * 128:(h + 1) * 128], in0=po[grp][:, h * 128:(h + 1) * 128],
                                                           scalar1=gsc[:, a:a + 1], scalar2=gbi[:, a:a + 1], op0=ALU.mult, op1=ALU.add),
                                 r=[po[grp].t, gsc.t, gbi.t], w=[hout.t])
                    v_op(lambda e: e.tensor_tensor(out=hm_bf[:, grp * 512:(grp + 1) * 512], in0=hout[:], in1=gate[:, grp * 512:(grp + 1) * 512], op=ALU.mult),
                         r=[hout.t, gate.t], w=[hm_bf.t])
                    rel(po[grp])
                    yield
                pt = nb()
                ptb = bfv(pt)
                for k in range(8):
                    tr(ptb[:, k * 128:(k + 1) * 128], hm_bf[:, k * 128:(k + 1) * 128], identb[:], r=[hm_bf.t, identb.t], w=[pt.t], sig=(k == 7))
                v_op(lambda e: e.tensor_tensor(out=mixT[:], in0=ptb[:, :].rearrange("p (k n) -> p k n", k=8),
                                               in1=gfm[:, 8:16].unsqueeze(2).broadcast_to([128, 8, 128]), op=ALU.mult),
                     r=[pt.t, gfm.t], w=[mixT.t])
                yield
                pa, pb2 = nb(True), nb(True)
                for k in range(8):
                    wb = load_wo(k)
                    mm(pa[:, :], mixT[:, k, :], wb[:, 0:512], k == 0, k == 7, r=[mixT.t, wb.t], w=[pa.t], sig=False)
                    mm(pb2[:, :], mixT[:, k, :], wb[:, 512:1024], k == 0, k == 7, r=[mixT.t, wb.t], w=[pb2.t], sig=True)
                    wo_ring.append(wb)
                    wo_prefetch(1)
                    if k == 3:
                        yield
                a_op(lambda e: e.activation(out=gtmpb[:], in_=pa[:, :], func=AF.Square, accum_out=ssb[:, 1:2]), r=[pa.t], w=[gtmpb.t, ssb.t])
                a_op(lambda e: e.activation(out=gtmpb[:], in_=pb2[:, :], func=AF.Square, accum_out=ssb[:, 2:3]), r=[pb2.t], w=[gtmpb.t, ssb.t])
                v_op(lambda e: e.tensor_tensor(out=ssb[:, 1:2], in0=ssb[:, 1:2], in1=ssb[:, 2:3], op=ALU.add), r=[ssb.t], w=[ssb.t])
                rstd_pow(rsb, rsb[:, 1:2], ssb, ssb[:, 1:2], 1, 1.0 / D)
                yield
                for i, pbx in enumerate((pa, pb2)):
                    v_op(lambda e: e.scalar_tensor_tensor(out=gtmpb[:], in0=pbx[:, :], scalar=rsb[:, 1:2],
                                                          in1=gpost[:, i * 512:(i + 1) * 512], op0=ALU.mult, op1=ALU.mult),
                         r=[pbx.t, rsb.t, gpost.t], w=[gtmpb.t])
                    p_op(lambda e: e.tensor_tensor(out=xs[:, i * 512:(i + 1) * 512], in0=xs[:, i * 512:(i + 1) * 512], in1=gtmpb[:], op=ALU.add),
                         r=[x1t[t], gtmpb.t], w=[x1t[t]])
                    rel(pbx)
                if "tail" in res:
                    res["tail"]()
                yield

            SCHED = "NbpbpbbbpbpbbpppppbnpbpBb"

            def merged(gb, gp, nt):
                gens = {"b": gb, "p": gp}
                cnt = {"b": 0, "p": 0}
                for c in SCHED:
                    cc_ = c.lower()
                    if cc_ in cnt:
                        Sched.label = "%s%d(nt=%s)" % (cc_, cnt[cc_], nt)
                        cnt[cc_] += 1
                    else:
                        Sched.label = c + "(nt=%s)" % nt
                    if c == "N":
                        if nt is not None:
                            nstage(nt)
                    elif c == "n":
                        if nt is not None:
                            nstage2(nt)
                    elif c == "B":
                        c = "b"
                        if gens[c] is not None:
                            try:
                                next(gens[c])
                            except StopIteration:
                                gens[c] = None
                    elif gens[c] is not None:
                        try:
                            next(gens[c])
                        except StopIteration:
                            gens[c] = None
                for g in gens.values():
                    if g is not None:
                        for _ in g:
                            pass

            def run_seq(*gens):
                for g in gens:
                    for _ in g:
                        pass

            def interleave(ga, gb):
                gens = [g for g in (ga, gb) if g is not None]
                while gens:
                    for g in list(gens):
                        try:
                            next(g)
                        except StopIteration:
                            gens.remove(g)

            with contextlib.ExitStack() as s2:
                qpad = sb(s2, "qpad", [128, 16, 128], BF16)
                vpads = [sb(s2, "vpad%d" % i, [128, 16, 128], BF16) for i in range(2)]
                p_op(lambda e: e.memset(qpad[:], 0.0), w=[qpad.t])
                pstride = qpad.ap[:, :, :].ap[0][0]
                qdiag = bass.AP(qpad.ap[:, :, :].tensor, 0, [[pstride, 128], [136, 16], [1, 8]])
                Sbf = [sb(s2, "Sbf%d" % i, [128, 4, 128], BF16) for i in range(2)]
                m0T = sb(s2, "m0T", [4, 16])
                m0x = sb(s2, "m0x", [128, 4])
                nx = sb(s2, "nx", [128, 4, 128])
                nq = sb(s2, "nq", [128, 4])
                k16 = sb(s2, "k16", [16, 4])
                wlm = sb(s2, "wlm", [128, 4, 16], BF16)
                ddx = sb(s2, "ddx", [128, 4])
                kd = sb(s2, "kd", [128, 4, 16])
                kbc = sb(s2, "kbc", [128, 4, 16])
                ones_ = sb(s2, "ones", [128, 128])
                pdsb = sb(s2, "pdsb", [128, 4])
                stage = X1.ap[:, 0:16, :].rearrange("p t c -> p (t c)")
                Sin4 = stage[:, 0:8192].rearrange("p (j h v) -> p j h v", h=4, v=128)
                Cin4 = stage[:, 8192:16384].rearrange("p (j h v) -> p j h v", h=4, v=128)
                Sin_t = [T("Sin%d" % j) for j in range(16)]
                Cin_t = [T("Cin%d" % j) for j in range(16)]
                S.dma(QS, X1[:, ST, :], x_d[ST * 128:(ST + 1) * 128, :], w=[x1t[ST]])
                with nc.allow_non_contiguous_dma(reason="tiny transposed state loads"):
                    S.dma(QS, m0T[:], sm_d.rearrange("j h -> h j"), w=[m0T.t])
                S.dma(QS, m0x[:], bass.AP(sm_d.tensor, 0, [[4, 16], [0, 8], [1, 4]]), w=[m0x.t])
                S.dma(QS, nx[:].rearrange("p h d -> p (h d)"), bass.AP(sn_d.tensor, 0, [[512, 16], [0, 8], [1, 512]]), w=[nx.t])
                S.dma(QS, n16[:], sn_d[:, :, :], w=[n16.t])
                for j in range(16):
                    S.dma(QS, Sin4[:, j, :, :], sret_d[j].rearrange("h d v -> d h v"), r=[w_in_t[7][1]] if j == 0 else [],
                          w=[Sin_t[j], stg_t[(j // 2) // 2]])
                    S.dma(QS, Cin4[:, j, :, :], sC_d[j].rearrange("h v d -> v h d"), w=[Cin_t[j], stg_t[(8 + j // 2) // 2]])
                v_op(lambda e: e.memset(ones_[:], 1.0), w=[ones_.t])
                sbn = [0]
                cd8 = [g ** 8 for g in GAM]

                def heads_sample(t, cur, res):
                    km, kz, v_r, vE, qkT, e12, ea_bf, mrow = (cur[n] for n in ("km", "kz", "v_r", "vE", "qkT", "e12", "ea_bf", "mrow"))
                    v_op(lambda e: e.tensor_tensor(out=kd[:], in0=e12[:, 8:12].unsqueeze(2).broadcast_to([128, 4, 16]),
                                                   in1=cst[:, C_FI:C_FI + 16].unsqueeze(1).broadcast_to([128, 4, 16]), op=ALU.mult),
                         r=[e12.t, cst.t], w=[kd.t])
                    pk = nb()
                    mm(pk[:, 0:64], ones_[:], kd[:].rearrange("p h j -> p (h j)"), True, True, r=[ones_.t, kd.t], w=[pk.t], sig=True)
                    v_op(lambda e: e.tensor_copy(kbc[:].rearrange("p h j -> p (h j)"), pk[:, 0:64]), r=[pk.t], w=[kbc.t])
                    pk = nb()
                    mm(pk[0:16, 0:4], cst[:, C_FI:C_FI + 16], e12[:, 8:12], True, True, r=[cst.t, e12.t], w=[pk.t], sig=True)
                    v_op(lambda e: e.tensor_copy(k16[:], pk[0:16, 0:4]), r=[pk.t], w=[k16.t])
                    for h in range(4):
                        v_op(lambda e: e.scalar_tensor_tensor(out=hout[:, 0:128], in0=qkf[:, 2, h * 128:(h + 1) * 128], scalar=1.0, in1=nx[:, h, :],
                                                              op0=ALU.mult, op1=ALU.mult, accum_out=nq[:, h:h + 1]),
                             r=[qkf.t, nx.t], w=[hout.t, nq.t])
                    pos = [nb(True), nb(True)]
                    pdd = nb(True)
                    for grp in range(2):
                        qo = 0 if grp == 0 else 8
                        vsrc = v_r if grp == 0 else vE
                        for h in range(4):
                            v_op(lambda e: e.tensor_copy(qdiag, qkT[:, qo + h, :].rearrange("p (j i) -> p j i", i=8)),
                                 r=[qkT.t], w=[qpad.t])
                            o_ap = pos[grp][:, h * 128:(h + 1) * 128]
                            mm(o_ap, sT[grp][:, h, :], vsrc[:, h, :], True, False, r=[sT[grp].t, vsrc.t], w=[pos[grp].t])
                            for j4 in range(4):
                                sb_ = Sbf[sbn[0] % 2]
                                sbn[0] += 1
                                jt = [j4 * 4 + i for i in range(4)]
                                if grp == 0:
                                    a_op(lambda e: e.copy(sb_[:], Sin4[:, j4 * 4:(j4 + 1) * 4, h, :]),
                                         r=[Sin_t[j] for j in jt], w=[sb_.t])
                                else:
                                    pc = nb()
                                    for i, j in enumerate(jt):
                                        tr(pc[:, i * 128:(i + 1) * 128], Cin4[:, j, h, :], ident,
                                           r=[Cin_t[j], identf.t], w=[pc.t], sig=(i == 3))
                                    v_op(lambda e: e.tensor_tensor(out=sb_[:], in0=pc[:, :].rearrange("p (j v) -> p j v", j=4),
                                                                   in1=kbc[:, h, j4 * 4:(j4 + 1) * 4].unsqueeze(2).broadcast_to([128, 4, 128]), op=ALU.mult),
                                         r=[pc.t, kbc.t], w=[sb_.t])
                                for i, j in enumerate(jt):
                                    last = (j == 15)
                                    mm(o_ap, qpad[:, j, :], sb_[:, i, :], False, last, r=[qpad.t, sb_.t], w=[pos[grp].t], sig=(last or i == 3))
                    for h in range(4):
                        mm(pdd[:, h:h + 1], sT[1][:, h, :], ea_bf[:, h:h + 1], True, True, r=[sT[1].t, ea_bf.t], w=[pdd.t], sig=(h == 3))
                    v_op(lambda e: e.tensor_tensor(out=ddx[:], in0=nq[:], in1=e12[:, 8:12], op=ALU.mult), r=[nq.t, e12.t], w=[ddx.t])
                    v_op(lambda e: e.tensor_tensor(out=pdsb[:], in0=pdd[:, 0:4], in1=ddx[:], op=ALU.add), r=[pdd.t, ddx.t], w=[pdsb.t])
                    rel(pdd)
                    for grp in range(2):
                        vsrc = v_r if grp == 0 else vE
                        for h in range(4):
                            vpad = vpads[h % 2]
                            S.op(POOL if h % 2 == 0 else DVE,
                                 lambda e: e.tensor_tensor(out=vpad[:], in0=vsrc[:, h, :].unsqueeze(1).broadcast_to([128, 16, 128]),
                                                           in1=cst[:, C_RM:C_RM + 16].unsqueeze(2).broadcast_to([128, 16, 128]), op=ALU.mult),
                                 r=[vsrc.t, cst.t], w=[vpad.t])
                            for j4 in range(4):
                                pu = nb()
                                jt = [j4 * 4 + i for i in range(4)]
                                if grp == 0:
                                    mm(pu[:, :], kz[:, h, :], vpad[:, j4 * 4:(j4 + 1) * 4, :].rearrange("p j v -> p (j v)"), True, True,
                                       r=[kz.t, vpad.t], w=[pu.t], sig=True)
                                    for i, j in enumerate(jt):
                                        v_op(lambda e: e.scalar_tensor_tensor(out=Sin4[:, j, h, :], in0=Sin4[:, j, h, :], scalar=cd8[h],
                                                                              in1=pu[:, i * 128:(i + 1) * 128], op0=ALU.mult, op1=ALU.add),
                                             r=[pu.t, Sin_t[j]], w=[Sin_t[j]])
                                else:
                                    for i, j in enumerate(jt):
                                        mm(pu[:, i * 128:(i + 1) * 128], vpad[:, j, :], km[:, h * 128:(h + 1) * 128], True, True,
                                           r=[vpad.t, km.t], w=[pu.t], sig=(i == 3))
                                    for i, j in enumerate(jt):
                                        v_op(lambda e: e.scalar_tensor_tensor(out=Cin4[:, j, h, :], in0=Cin4[:, j, h, :],
                                                                              scalar=kbc[:, h, j:j + 1],
                                                                              in1=pu[:, i * 128:(i + 1) * 128], op0=ALU.mult, op1=ALU.add),
                                             r=[pu.t, Cin_t[j], kbc.t], w=[Cin_t[j]])
                    v_op(lambda e: e.tensor_tensor(out=wlm[:], in0=e12[:, 0:4].unsqueeze(2).broadcast_to([128, 4, 16]),
                                                   in1=cst[:, C_RM:C_RM + 16].unsqueeze(1).broadcast_to([128, 4, 16]), op=ALU.mult),
                         r=[e12.t, cst.t], w=[wlm.t])
                    pn = nb()
                    for h in range(4):
                        mm(pn[0:16, h * 128:(h + 1) * 128], wlm[:, h, :], km[:, h * 128:(h + 1) * 128], True, True,
                           r=[wlm.t, km.t], w=[pn.t], sig=(h == 3))
                    for h in range(4):
                        v_op(lambda e: e.scalar_tensor_tensor(out=n16[:, h, :], in0=n16[:, h, :], scalar=k16[:, h:h + 1],
                                                              in1=pn[0:16, h * 128:(h + 1) * 128], op0=ALU.mult, op1=ALU.add),
                             r=[n16.t, k16.t, pn.t], w=[n16.t])
                    def stores():
                        S.dma(QS, ns_d[:, :, :], n16[:], r=[n16.t])
                        with nc.allow_non_contiguous_dma(reason="tiny m state rows"):
                            S.dma(QS, ms_d[:, :], mrow.ap[7:128:8, :], r=[mrow.t])
                    deferred.append(stores)
                    res["po0"], res["po1"], res["pdd"] = pos[0], pos[1], pdsb
                    yield

                deferred = []
                wo_prefetch(3)
                nstage(ST)
                nstage2(ST)
                run_seq(front(ST, sets[0], {"m0T": m0T, "m0x": m0x}), back(ST, sets[0], heads_sample))
                for fn in deferred:
                    fn()
                for j in range(NJ):
                    wup_casts.append(("d", j, 0))
                for k in range(8):
                    for hc in range(4):
                        wup_casts.append(("u", k, hc))
                S.barrier(dma=False)

            with contextlib.ExitStack() as s3:
                sets.append(mkset(s3, 1))
                S_f = sb(s3, "S_f", [128, 4, 128])
                S_bf = sb(s3, "S_bf", [128, 4, 128], BF16)
                C_f = sb(s3, "C_f", [128, 4, 128])
                C_bf = sb(s3, "C_bf", [128, 4, 128], BF16)
                n_f = sb(s3, "n_f", [128, 4])
                n_bf = sb(s3, "n_bf", [128, 4], BF16)
                ctr = sb(s3, "ctr", [128, 4, 128])
                ntr = sb(s3, "ntr", [4, 128])
                print("phaseA prompt sbuf remaining", nc.sbuf_bytes_remaining)
                cd = [g ** 128 for g in GAM]
                S.dma(QS, cst[:], cst_d[0], w=[cst.t])
                wo_ring.append(wo3)
                wo_prefetch(1)

                def stage_alias(t):
                    if t < 8:
                        return [Sin_t[2 * t], Sin_t[2 * t + 1]]
                    return [Cin_t[2 * (t - 8)], Cin_t[2 * (t - 8) + 1]]

                def load_x(t):
                    for j in (2 * (t % 8), 2 * (t % 8) + 1):
                        if t < 8:
                            S.dma(QS, Ss_d[j].rearrange("h d v -> d h v"), Sin4[:, j, :, :], r=[Sin_t[j]])
                        else:
                            S.dma(QS, Cs_d[j].rearrange("h v d -> v h d"), Cin4[:, j, :, :], r=[Cin_t[j]])
                    S.dma(QS, X1[:, t, :], x_d[t * 128:(t + 1) * 128, :], w=[x1t[t]] + stage_alias(t))

                def heads_prompt(t, cur, res):
                    km, kz, v_r, vE, qkT, e12, ea_bf = (cur[n] for n in ("km", "kz", "v_r", "vE", "qkT", "e12", "ea_bf"))
                    po0, po1, pdd = nb(True), nb(True), nb(True)
                    res["po0"], res["po1"], res["pdd"] = po0, po1, pdd
                    for h in range(4):
                        o_ap = po0[:, h * 128:(h + 1) * 128]
                        mm(o_ap, sT[0][:, h, :], v_r[:, h, :], True, t == 0, r=[sT[0].t, v_r.t], w=[po0.t], sig=(t == 0 and h == 3))
                        if t > 0:
                            mm(o_ap, qkT[:, h, :], S_bf[:, h, :], False, True, r=[qkT.t, S_bf.t], w=[po0.t], sig=(h == 3))
                    pu = nb()
                    for h in range(4):
                        mm(pu[:, h * 128:(h + 1) * 128], kz[:, h, :], v_r[:, h, :], True, True, r=[kz.t, v_r.t], w=[pu.t], sig=(h == 3))
                    for h in range(4):
                        if t == 0:
                            v_op(lambda e: e.tensor_copy(S_f[:, h, :], pu[:, h * 128:(h + 1) * 128]), r=[pu.t], w=[S_f.t])
                        else:
                            v_op(lambda e: e.scalar_tensor_tensor(out=S_f[:, h, :], in0=S_f[:, h, :], scalar=cd[h],
                                                                  in1=pu[:, h * 128:(h + 1) * 128], op0=ALU.mult, op1=ALU.add),
                                 r=[pu.t, S_f.t], w=[S_f.t])
                    res["tail"] = lambda: a_op(lambda e: e.copy(S_bf[:], S_f[:]), r=[S_f.t], w=[S_bf.t])
                    yield
                    if t > 0:
                        v_op(lambda e: e.tensor_tensor(out=C_f[:], in0=C_f[:], in1=e12[:, 8:12].unsqueeze(2).broadcast_to([128, 4, 128]), op=ALU.mult),
                             r=[C_f.t, e12.t], w=[C_f.t])
                        v_op(lambda e: e.tensor_tensor(out=n_f[:], in0=n_f[:], in1=e12[:, 8:12], op=ALU.mult), r=[n_f.t, e12.t], w=[n_f.t])
                        v_op(lambda e: e.tensor_copy(C_bf[:], C_f[:]), r=[C_f.t], w=[C_bf.t])
                        v_op(lambda e: e.tensor_copy(n_bf[:], n_f[:]), r=[n_f.t], w=[n_bf.t])
                    for h in range(4):
                        o_ap = po1[:, h * 128:(h + 1) * 128]
                        mm(o_ap, sT[1][:, h, :], vE[:, h, :], True, t == 0, r=[sT[1].t, vE.t], w=[po1.t], sig=(t == 0 and h == 3))
                        if t > 0:
                            mm(o_ap, qkT[:, 8 + h, :], C_bf[:, h, :], False, True, r=[qkT.t, C_bf.t], w=[po1.t], sig=(h == 3))
                    for h in range(4):
                        mm(pdd[:, h:h + 1], sT[1][:, h, :], ea_bf[:, h:h + 1], True, t == 0, r=[sT[1].t, ea_bf.t], w=[pdd.t], sig=(t == 0 and h == 3))
                        if t > 0:
                            mm(pdd[:, h:h + 1], qkT[:, 8 + h, :], n_bf[:, h:h + 1], False, True, r=[qkT.t, n_bf.t], w=[pdd.t], sig=(h == 3))
                    pu = nb()
                    for h in range(4):
                        mm(pu[:, h * 128:(h + 1) * 128], km[:, h * 128:(h + 1) * 128], vE[:, h, :], True, True, r=[km.t, vE.t], w=[pu.t], sig=(h == 3))
                    pn = nb()
                    for h in range(4):
                        mm(pn[:, h:h + 1], km[:, h * 128:(h + 1) * 128], ea_bf[:, h:h + 1], True, True, r=[km.t, ea_bf.t], w=[pn.t], sig=(h == 3))
                    if t == 0:
                        v_op(lambda e: e.tensor_copy(C_f[:].rearrange("p h v -> p (h v)"), pu[:, :]), r=[pu.t], w=[C_f.t])
                        v_op(lambda e: e.tensor_copy(n_f[:], pn[:, 0:4]), r=[pn.t], w=[n_f.t])
                    else:
                        v_op(lambda e: e.tensor_tensor(out=C_f[:].rearrange("p h v -> p (h v)"), in0=C_f[:].rearrange("p h v -> p (h v)"), in1=pu[:, :], op=ALU.add),
                             r=[pu.t, C_f.t], w=[C_f.t])
                        v_op(lambda e: e.tensor_tensor(out=n_f[:], in0=n_f[:], in1=pn[:, 0:4], op=ALU.add), r=[pn.t, n_f.t], w=[n_f.t])
                    yield

                load_x(0)
                load_x(1)
                load_x(2)
                nstage(0)
                nstage2(0)
                run_seq(front(0, sets[0], {}))
                nstage(1)
                nstage2(1)
                for t in range(16):
                    issue_wup_casts(4)
                    if t + 3 < 16:
                        load_x(t + 3)
                    gF = front(t + 1, sets[(t + 1) % 2], {}) if t + 1 < 16 else None
                    merged(back(t, sets[t % 2], heads_prompt), gF, t + 2 if t + 2 < 16 else None)
                issue_wup_casts(100)
                mrow = sets[15 % 2]["mrow"]
                S.dma(QS, Sp_d.rearrange("h d v -> d h v"), S_f[:], r=[S_f.t])
                pc = nb()
                for h in range(4):
                    tr(pc[:, h * 128:(h + 1) * 128], C_f[:, h, :], ident, r=[C_f.t, identf.t], w=[pc.t], sig=(h == 3))
                v_op(lambda e: e.tensor_copy(ctr[:].rearrange("p h v -> p (h v)"), pc[:, :]), r=[pc.t], w=[ctr.t])
                S.dma(QS, Cp_d.rearrange("h v d -> v h d"), ctr[:], r=[ctr.t])
                pc = nb()
                tr(pc[0:4, 0:128], n_f[:], ident, r=[n_f.t, identf.t], w=[pc.t], sig=True)
                v_op(lambda e: e.tensor_copy(ntr[:], pc[0:4, 0:128]), r=[pc.t], w=[ntr.t])
                S.dma(QS, np_d[:, :], ntr[:], r=[ntr.t])
                S.dma(QS, mp_d[:, :], mrow[127:128, :], r=[mrow.t])
                S.barrier(dma=True)

        with contextlib.ExitStack() as sbk:
            w_up = sb(sbk, "w_up_bf", [128, 8, 2 * DFF], BF16)
            S.dma(QS, gpost[:], bc_rows(gpost_d, 1, D), w=[gpost.t])
            convp = sb(sbk, "convp", [128, 44, 4])
            S.dma(QS, convp[:].rearrange("p j r -> p (j r)"), convp_d[:, :], w=[convp.t])
            w_up_t = [[[T("w_up_%d_%d_%d" % (gv, b_, k)) for k in range(8)] for b_ in range(4)] for gv in range(2)]
            for b_, (j0, j1) in enumerate(jblk):
                for gv in range(2):
                    c0, c1 = gv * DFF + j0 * 128, gv * DFF + j1 * 128
                    for k in range(8):
                        S.dma(QS, w_up[:, k, c0:c1], wup_scr[k * 128:(k + 1) * 128, c0:c1],
                              r=[wups_t[k][hc] for hc in range(c0 // 1408, (c1 - 1) // 1408 + 1)], w=[w_up_t[gv][b_][k]])
            wd_ring = [sb(sbk, "wd%d" % i, [128, D], BF16) for i in range(3)]
            wd_n = [0]
            h2 = sb(sbk, "h2", [128, D], BF16)
            hbs = [sb(sbk, "h2T%d" % i, [128, 8, 258], BF16) for i in range(2)]
            cg = [sb(sbk, "cg%d" % i, [128, 256]) for i in range(2)]
            cv = [sb(sbk, "cv%d" % i, [128, 256]) for i in range(2)]
            hT = [sb(sbk, "hT%d" % i, [128, 256], BF16) for i in range(3)]
            evs = [sb(sbk, "ev%d" % i, [128, D]) for i in range(2)]
            histn = sb(sbk, "histn", [128, 44, 32])
            hist = sb(sbk, "hist", [128, 44, 32])
            ulp2 = sb(sbk, "ulp2", [128, 44, 2])
            cio = [sb(sbk, "cio%d" % i, [32, 512]) for i in range(2)]
            cio_n = [0]
            ring["base"], ring["n"], ring["i"] = 4, 4, 0
            acc = banks[0:4]

            def load_wd(j):
                b = wd_ring[wd_n[0] % 3]
                wd_n[0] += 1
                S.dma(QS, b[:], wdn_scr[j * 128:(j + 1) * 128, :], r=[wdn_t[j]], w=[b.t])
                return b

            def ffn_norm_T(t, hb, col0):
                xs = X1[:, t, :]
                a_op(lambda e: e.activation(out=h2[:], in_=xs, func=AF.Square, accum_out=ss[:, 0:1]), r=[x1t[t]], w=[h2.t, ss.t])
                rstd_pow(rs, rs[:, 0:1], ss, ss[:, 0:1], 1, 1.0 / D)
                v_op(lambda e: e.tensor_scalar(out=h2[:], in0=xs, scalar1=rs[:, 0:1], scalar2=None, op0=ALU.mult), r=[x1t[t], rs.t], w=[h2.t])
                pt = nb()
                ptb = bfv(pt)
                for k in range(8):
                    tr(ptb[:, k * 128:(k + 1) * 128], h2[:, k * 128:(k + 1) * 128], identb[:], r=[h2.t, identb.t], w=[pt.t], sig=(k == 7))
                v_op(lambda e: e.tensor_tensor(out=hb[:, :, col0:col0 + 128], in0=ptb[:, :].rearrange("p (k n) -> p k n", k=8),
                                               in1=gfm[:, 16:24].unsqueeze(2).broadcast_to([128, 8, 128]), op=ALU.mult),
                     r=[pt.t, gfm.t], w=[hb.t])

            def ffn_out(i, t, a0, a1):
                xs = X1[:, t, :]
                ev = evs[i]
                lk = [T("lock0"), T("lock1")]
                a_op(lambda e: e.activation(out=h2[:, 0:512], in_=a0[:, :], func=AF.Square, accum_out=ss[:, 1:2]), r=[a0.t], w=[h2.t, ss.t, lk[0]])
                v_op(lambda e: e.tensor_copy(ev[:, 512:1024], a1[:, :]), r=[a1.t], w=[ev.t, lk[1]])
                a_op(lambda e: e.activation(out=h2[:, 0:512], in_=a1[:, :], func=AF.Square, accum_out=ss[:, 2:3]), r=[a1.t], w=[h2.t, ss.t, lk[1]])
                v_op(lambda e: e.tensor_copy(ev[:, 0:512], a0[:, :]), r=[a0.t], w=[ev.t, lk[0]])
                v_op(lambda e: e.tensor_tensor(out=ss[:, 1:2], in0=ss[:, 1:2], in1=ss[:, 2:3], op=ALU.add), r=[ss.t], w=[ss.t])
                rstd_pow(rs, rs[:, 1:2], ss, ss[:, 1:2], 1, 1.0 / D)
                v_op(lambda e: e.scalar_tensor_tensor(out=ev[:], in0=ev[:], scalar=rs[:, 1:2], in1=gpost[:], op0=ALU.mult, op1=ALU.mult),
                     r=[ev.t, rs.t, gpost.t], w=[ev.t])
                p_op(lambda e: e.tensor_tensor(out=xs, in0=xs, in1=ev[:], op=ALU.add), r=[x1t[t], ev.t], w=[x1t[t]])
                S.dma(QP, y_d[t * 128:(t + 1) * 128, :], xs, r=[x1t[t]])

            def load_cache():
                for c4 in range(11):
                    ci = cio[cio_n[0] % 2]
                    cio_n[0] += 1
                    S.dma(QS, ci[:], sconv_d[:, c4 * 512:(c4 + 1) * 512], w=[ci.t])
                    pt = nb()
                    for i in range(4):
                        tr(pt[:, i * 32:(i + 1) * 32], ci[:, i * 128:(i + 1) * 128], identf[0:32, 0:32],
                           r=[ci.t, identf.t], w=[pt.t], sig=(i == 3))
                    a_op(lambda e: e.copy(hist[:, c4 * 4:(c4 + 1) * 4, :].rearrange("p c r -> p (c r)"), pt[:, 0:128]), r=[pt.t], w=[hist.t])

            hn = [0]
            print("phaseB sbuf remaining", nc.sbuf_bytes_remaining)
            def cache_precompute():
                h4 = hist[:].rearrange("p c (j r) -> p c j r", r=2)
                w0b = convp[:, :, 0:1].unsqueeze(3).broadcast_to([128, 44, 16, 1])
                w1b = convp[:, :, 1:2].unsqueeze(3).broadcast_to([128, 44, 16, 1])
                p_op(lambda e: e.tensor_tensor(out=h4[:, :, :, 0:1], in0=h4[:, :, :, 0:1], in1=w0b, op=ALU.mult), r=[hist.t, convp.t], w=[hist.t])
                p_op(lambda e: e.tensor_tensor(out=histn[:].rearrange("p c (j r) -> p c j r", r=2)[:, :, :, 0:1], in0=h4[:, :, :, 1:2], in1=w1b, op=ALU.mult),
                     r=[hist.t, convp.t], w=[histn.t])
                p_op(lambda e: e.tensor_tensor(out=h4[:, :, :, 0:1], in0=h4[:, :, :, 0:1], in1=histn[:].rearrange("p c (j r) -> p c j r", r=2)[:, :, :, 0:1], op=ALU.add),
                     r=[hist.t, histn.t], w=[hist.t])
                p_op(lambda e: e.tensor_tensor(out=h4[:, :, :, 1:2], in0=h4[:, :, :, 1:2], in1=w0b, op=ALU.mult), r=[hist.t, convp.t], w=[hist.t])

            def up_stage(g, j, hb, tiles, N, sample):
                NW = N + 2
                blk = [b_ for b_, (j0, j1) in enumerate(jblk) if j0 <= j < j1][0]
                res = []
                for gv in range(2):
                    pu = nb()
                    c0 = gv * DFF + j * 128
                    for k in range(8):
                        mm(pu[:, 0:NW], w_up[:, k, c0:c0 + 128], hb[:, k, 0:NW], k == 0, k == 7,
                           r=[hb.t, w_up_t[gv][blk][k]], w=[pu.t], sig=(k == 7))
                    dst = (cg if gv == 0 else cv)[j % 2]
                    cc = gv * NJ + j
                    w0, w1, w2, bb = (convp[:, cc, r:r + 1] for r in range(4))
                    if not sample:
                        a_op(lambda e: e.activation(out=dst[:, 0:N], in_=pu[:, 2:2 + N], func=AF.Identity, scale=w2, bias=bb),
                             r=[pu.t, convp.t], w=[dst.t])
                        v_op(lambda e: e.scalar_tensor_tensor(out=dst[:, 0:N], in0=pu[:, 1:1 + N], scalar=w1, in1=dst[:, 0:N], op0=ALU.mult, op1=ALU.add),
                             r=[pu.t, convp.t, dst.t], w=[dst.t])
                        v_op(lambda e: e.scalar_tensor_tensor(out=dst[:, 0:N], in0=pu[:, 0:N], scalar=w0, in1=dst[:, 0:N], op0=ALU.mult, op1=ALU.add),
                             r=[pu.t, convp.t, dst.t], w=[dst.t])
                        if g == 7:
                            v_op(lambda e: e.tensor_copy(ulp2[:, cc, :], pu[:, N:N + 2]), r=[pu.t, dst.t], w=[ulp2.t])
                    else:
                        u3 = pu[:, 2:2 + N].rearrange("p (j i) -> p j i", i=8)
                        d3 = dst[:, 0:N].rearrange("p (j i) -> p j i", i=8)
                        h3 = hist[:, cc, :].rearrange("p (j r) -> p j r", r=2)
                        a_op(lambda e: e.activation(out=dst[:, 0:N], in_=pu[:, 2:2 + N], func=AF.Identity, scale=w2, bias=bb),
                             r=[pu.t, convp.t], w=[dst.t])
                        v_op(lambda e: e.scalar_tensor_tensor(out=d3[:, :, 1:8], in0=u3[:, :, 0:7], scalar=w1, in1=d3[:, :, 1:8], op0=ALU.mult, op1=ALU.add),
                             r=[pu.t, convp.t, dst.t], w=[dst.t])
                        v_op(lambda e: e.scalar_tensor_tensor(out=d3[:, :, 2:8], in0=u3[:, :, 0:6], scalar=w0, in1=d3[:, :, 2:8], op0=ALU.mult, op1=ALU.add),
                             r=[pu.t, convp.t, dst.t], w=[dst.t])
                        v_op(lambda e: e.tensor_tensor(out=d3[:, :, 0:2], in0=d3[:, :, 0:2], in1=h3[:, :, 0:2], op=ALU.add),
                             r=[hist.t, dst.t], w=[dst.t])
                        v_op(lambda e: e.tensor_copy(histn[:, cc, :].rearrange("p (j r) -> p j r", r=2), u3[:, :, 6:8]), r=[pu.t, dst.t], w=[histn.t])
                    res.append(dst)
                gb, vb = res
                a_op(lambda e: e.activation(out=gb[:, 0:N], in_=gb[:, 0:N], func=AF.Gelu_apprx_tanh), r=[gb.t], w=[gb.t])
                hj = hT[hn[0] % 3]
                hn[0] += 1
                p_op(lambda e: e.tensor_tensor(out=hj[:, 0:N], in0=gb[:, 0:N], in1=vb[:, 0:N], op=ALU.mult), r=[gb.t, vb.t], w=[hj.t])
                return hj

            def down_stage(j, hj, tiles):
                wd = load_wd(j)
                for i, t in enumerate(tiles):
                    for c in range(2):
                        a = acc[i * 2 + c]
                        mm(a[:, :], hj[:, i * 128:(i + 1) * 128], wd[:, c * 512:(c + 1) * 512], j == 0, j == NJ - 1,
                           r=[hj.t, wd.t], w=[a.t], sig=(j == NJ - 1))

            def group_tail(g, tiles, sample):
                for i, t in enumerate(tiles):
                    ffn_out(i, t, acc[i * 2], acc[i * 2 + 1])
                if g == 7 or sample:
                    nr = 32 if sample else 2
                    for c4 in range(11):
                        pt = nb()
                        for i in range(4):
                            c = c4 * 4 + i
                            src = histn[:, c, :] if sample else ulp2[:, c, :]
                            tr(pt[0:nr, i * 128:(i + 1) * 128], src, ident, r=[histn.t if sample else ulp2.t, identf.t], w=[pt.t], sig=(i == 3))
                        co = cio[cio_n[0] % 2]
                        cio_n[0] += 1
                        a_op(lambda e: e.copy(co[0:nr, :], pt[0:nr, :]), r=[pt.t], w=[co.t])
                        if sample:
                            S.dma(QS, cvs_d[:, c4 * 512:(c4 + 1) * 512], co[:, :], r=[co.t])
                        else:
                            S.dma(QS, cvp_d[:, c4 * 512:(c4 + 1) * 512], co[0:2, :], r=[co.t])

            def group_head(g):
                sample = (g == 8)
                tiles = [ST] if sample else [2 * g, 2 * g + 1]
                N = 128 * len(tiles)
                hb = hbs[g % 2]
                hprev = hbs[(g + 1) % 2]
                if sample or g == 0:
                    v_op(lambda e: e.memset(hb[:, :, 0:2], 0.0), w=[hb.t])
                else:
                    v_op(lambda e: e.tensor_copy(hb[:, :, 0:2], hprev[:, :, 256:258]), r=[hprev.t], w=[hb.t])
                for i, t in enumerate(tiles):
                    ffn_norm_T(t, hb, 2 + i * 128)
                return hb, tiles, N, sample

            pend = []

            def flush_one():
                pg, pj, phj, ptiles, psample = pend.pop(0)
                down_stage(pj, phj, ptiles)
                if pj == NJ - 1:
                    group_tail(pg, ptiles, psample)

            nxt = group_head(0)
            for g in range(9):
                hb, tiles, N, sample = nxt
                for j in range(NJ):
                    hj = up_stage(g, j, hb, tiles, N, sample)
                    pend.append((g, j, hj, tiles, sample))
                    if len(pend) > 2:
                        flush_one()
                    if j == 8 and g + 1 < 9:
                        nxt = group_head(g + 1)
                    if g == 1 and j == 14:
                        load_cache()
                        cache_precompute()
            while pend:
                flush_one()
            S.barrier(dma=True)
    return nc


def _consts():
    c = np.zeros((2, 128, NCONST), np.float32)
    idx = np.arange(128)
    for ty in range(2):
        c[ty, :, C_ID:C_ID + 128] = np.eye(128, dtype=np.float32)
        if ty == 0:
            same = np.ones((128, 128), bool)
            loc = idx
            clen = 128
            c[ty, 127, C_SEL:C_SEL + 128] = 1.0
        else:
            same = (idx[:, None] // 8) == (idx[None, :] // 8)
            loc = idx % 8
            clen = 8
        mask = (idx[None, :] >= idx[:, None]) & same
        c[ty, :, C_MT:C_MT + 128] = mask
        for h in range(4):
            lg = np.log(np.float64(GAM[h]))
            R = np.exp(-(loc[:, None] + 1.0) * lg) * mask
            c[ty, :, C_R + h * 128:C_R + (h + 1) * 128] = R
            c[ty, :, C_XI + h * 128:C_XI + (h + 1) * 128] = np.exp((loc[None, :] + 1.0) * lg)
            c[ty, :, C_Z + h] = np.exp((clen - 1.0 - loc) * lg)
        c[ty, :, C_RM:C_RM + 16] = (idx[:, None] // 8) == np.arange(16)[None, :]
        c[ty, :, C_FI:C_FI + 16] = idx[:, None] == 8 * np.arange(16)[None, :]
    return c


def _rot():
    freqs = 10000.0 ** (-np.arange(0, 128, 2, dtype=np.float64) / 128)
    r = np.zeros((NT, 128, 256), np.float32)
    for t in range(NT):
        if t == ST:
            pos = (16384 + (np.arange(128) % 8)).astype(np.float64)
        else:
            pos = (t * 128 + np.arange(128)).astype(np.float64)
        ang = pos[:, None] * freqs[None, :]
        cs, sn = np.cos(ang), np.sin(ang)
        r[t, :, 0:64] = cs
        r[t, :, 64:128] = sn
        r[t, :, 128:192] = cs * SCALE
        r[t, :, 192:256] = sn * SCALE
    return r


_CACHE = {}


def _prepare(x_prompt, x_sample, state_ret, state_mlstm_C, state_mlstm_n, state_mlstm_m,
             cache_ffn_conv, pre_mix_gain, w_in, b_gates, ret_head_gain, mlstm_head_gain,
             w_out, post_mix_gain, pre_ffn_gain, w_up, conv_w, conv_b, w_down, post_ffn_gain, cores=range(8)):
    f = lambda a: np.ascontiguousarray(np.asarray(a, dtype=np.float32))
    xp, xs = f(x_prompt), f(x_sample)
    hg = np.concatenate([f(ret_head_gain)[0], f(mlstm_head_gain)[0]])
    gfm = np.concatenate([f(pre_mix_gain)[0].reshape(8, 128).T, hg.reshape(8, 128).T,
                          f(pre_ffn_gain)[0].reshape(8, 128).T], axis=1)
    gpost = np.stack([f(post_mix_gain)[0], f(post_ffn_gain)[0]])
    cw, cb = f(conv_w)[0], f(conv_b)[0]
    convp = np.stack([cw[0], cw[1], cw[2], cb], axis=-1).reshape(44, 128, 4).transpose(1, 0, 2).reshape(128, 176)
    bm = np.tile(((np.arange(128)[None, :] // 8) == np.arange(16)[:, None]).astype(np.float32).reshape(1, 2048), (128, 1))
    shared = dict(w_in=f(w_in)[0], w_out=f(w_out)[0], w_up=f(w_up)[0], w_down=f(w_down)[0],
                  gfm=np.ascontiguousarray(gfm), gpost=gpost, bg=f(b_gates), convp=np.ascontiguousarray(convp),
                  consts=_consts(), bm=bm, rot=_rot())
    sr, sc, sn, sm, cc = f(state_ret)[0], f(state_mlstm_C)[0], f(state_mlstm_n)[0], f(state_mlstm_m)[0], f(cache_ffn_conv)[0]
    in_maps = []
    for c in cores:
        sl = slice(16 * c, 16 * c + 16)
        m = dict(shared)
        m["x"] = np.concatenate([xp[c], xs[sl].reshape(128, D)], axis=0)
        m["sret"], m["sC"], m["sn"], m["sm"] = sr[sl], sc[sl], sn[sl], sm[sl]
        m["sconv"] = np.ascontiguousarray(cc[sl].reshape(32, 2 * DFF))
        in_maps.append(m)
    return in_maps


def _assemble(res):
    g = lambda n: [np.asarray(r[n]) for r in res]
    y = g("y")
    outs = (
        np.stack([a[:2048] for a in y]),
        np.concatenate([a[2048:].reshape(16, 8, D) for a in y]),
        np.stack(g("Sp"))[None], np.concatenate(g("Ss"))[None],
        np.stack(g("Cp"))[None], np.concatenate(g("Cs"))[None],
        np.stack(g("np"))[None], np.concatenate(g("ns"))[None],
        np.stack([a[0] for a in g("mp")])[None], np.concatenate(g("ms"))[None],
        np.stack(g("cvp"))[None], np.concatenate([a.reshape(16, 2, 2 * DFF) for a in g("cvs")])[None],
    )
    return tuple(np.ascontiguousarray(o, dtype=np.float32) for o in outs)


def kernel(**inputs):
    if "nc" not in _CACHE:
        _CACHE["nc"] = build()
    in_maps = _prepare(**inputs)
    res = run_bass_kernel_spmd(_CACHE["nc"], in_maps, core_ids=list(range(8))).results
    return _assemble(res)
```
